# Optimizing a Trainium2 kernel written in Bass

```python
import math
import jax, jax.numpy as jnp
from jax import lax
import numpy as np

D_MODEL = 1024
BATCH = 4
SEQ = 8192
DEPTH = 2

N_MIXERS = 2
N_HEADS = 8
HEAD_DIM = 64
V_DIM = 2 * HEAD_DIM
D_FF = 2816
CONV_WIDTH = 3
Q_BLOCK = 128
EPS = 1e-5
N_ATTN = (DEPTH + 1) // 2
N_CONV = DEPTH // 2

kernel_name = "hybrid_diffattn_shortconv_macaron"


def rmsnorm(x, g):
    xf = x.astype(jnp.float32)
    y = xf * lax.rsqrt(jnp.mean(xf * xf, axis=-1, keepdims=True) + EPS)
    return (y * g.astype(jnp.float32)).astype(x.dtype)


def swiglu(h, w_gate, w_up, w_down):
    return (jax.nn.silu(h @ w_gate) * (h @ w_up)) @ w_down


def diff_attention(h, w_qkv, lq1, lk1, lq2, lk2, subln, w_out, layer_idx):
    b, s, _ = h.shape
    qkv = h @ w_qkv
    q, k, v = jnp.split(qkv, 3, axis=-1)
    q = q.reshape(b, s, N_HEADS, 2, HEAD_DIM) * (HEAD_DIM ** -0.5)
    k = k.reshape(b, s, N_HEADS, 2, HEAD_DIM)
    v = v.reshape(b, s, N_HEADS, V_DIM)
    lambda_init = 0.8 - 0.6 * math.exp(-0.3 * layer_idx)
    lam = (jnp.exp(jnp.sum(lq1.astype(jnp.float32) * lk1.astype(jnp.float32)))
           - jnp.exp(jnp.sum(lq2.astype(jnp.float32) * lk2.astype(jnp.float32)))
           + lambda_init)
    nb = s // Q_BLOCK
    q_blocks = q.reshape(b, nb, Q_BLOCK, N_HEADS, 2, HEAD_DIM).transpose(1, 0, 2, 3, 4, 5)
    k_pos = jnp.arange(s)

    def one_block(args):
        q_blk, blk = args
        scores = jnp.einsum('bqhcd,bkhcd->bhcqk', q_blk, k).astype(jnp.float32)
        q_pos = blk * Q_BLOCK + jnp.arange(Q_BLOCK)
        causal = k_pos[None, :] <= q_pos[:, None]
        scores = jnp.where(causal, scores, -jnp.inf)
        p = jax.nn.softmax(scores, axis=-1)
        a = p[:, :, 0] - lam * p[:, :, 1]
        return jnp.einsum('bhqk,bkhe->bqhe', a.astype(v.dtype), v)

    o = lax.map(one_block, (q_blocks, jnp.arange(nb)))
    o = o.transpose(1, 0, 2, 3, 4).reshape(b, s, N_HEADS, V_DIM)
    o = rmsnorm(o, subln) * (1.0 - lambda_init)
    return o.reshape(b, s, N_HEADS * V_DIM) @ w_out


def short_conv(h, w_in, w_conv, w_out):
    s = h.shape[1]
    gb, gc, u = jnp.split(h @ w_in, 3, axis=-1)
    u = gc * u
    u_pad = jnp.pad(u, ((0, 0), (CONV_WIDTH - 1, 0), (0, 0)))
    y = sum(w_conv[j] * u_pad[:, j:j + s] for j in range(CONV_WIDTH))
    return (gb * y) @ w_out


def setup_inputs(seed: int = 0) -> dict:
    key = jax.random.key(seed)
    ks = jax.random.split(key, 24)
    D, F = D_MODEL, D_FF
    nrm = lambda k, shape, fan_in: jax.random.normal(k, shape, jnp.float32) * fan_in ** -0.5
    gain = lambda k, shape: 1.0 + 0.02 * jax.random.normal(k, shape, jnp.float32)
    return {
        "x": jax.random.normal(ks[0], (BATCH, SEQ, D), jnp.float32),
        "ffn1_norm": gain(ks[1], (DEPTH, D)),
        "ffn1_w_gate": nrm(ks[2], (DEPTH, D, F), D),
        "ffn1_w_up": nrm(ks[3], (DEPTH, D, F), D),
        "ffn1_w_down": nrm(ks[4], (DEPTH, F, D), F),
        "mix_norm": gain(ks[5], (DEPTH, D)),
        "attn_w_qkv": nrm(ks[6], (N_ATTN, D, 3 * D), D),
        "attn_lambda_q1": 0.1 * jax.random.normal(ks[7], (N_ATTN, HEAD_DIM), jnp.float32),
        "attn_lambda_k1": 0.1 * jax.random.normal(ks[8], (N_ATTN, HEAD_DIM), jnp.float32),
        "attn_lambda_q2": 0.1 * jax.random.normal(ks[9], (N_ATTN, HEAD_DIM), jnp.float32),
        "attn_lambda_k2": 0.1 * jax.random.normal(ks[10], (N_ATTN, HEAD_DIM), jnp.float32),
        "attn_subln": gain(ks[11], (N_ATTN, V_DIM)),
        "attn_w_out": nrm(ks[12], (N_ATTN, D, D), D),
        "conv_w_in": nrm(ks[13], (N_CONV, D, 3 * D), D),
        "conv_w": nrm(ks[14], (N_CONV, CONV_WIDTH, D), CONV_WIDTH),
        "conv_w_out": nrm(ks[15], (N_CONV, D, D), D),
        "ffn2_norm": gain(ks[16], (DEPTH, D)),
        "ffn2_w_gate": nrm(ks[17], (DEPTH, D, F), D),
        "ffn2_w_up": nrm(ks[18], (DEPTH, D, F), D),
        "ffn2_w_down": nrm(ks[19], (DEPTH, F, D), F),
        "final_norm": gain(ks[20], (D,)),
    }


def reference(x, ffn1_norm, ffn1_w_gate, ffn1_w_up, ffn1_w_down, mix_norm,
              attn_w_qkv, attn_lambda_q1, attn_lambda_k1, attn_lambda_q2, attn_lambda_k2,
              attn_subln, attn_w_out, conv_w_in, conv_w, conv_w_out,
              ffn2_norm, ffn2_w_gate, ffn2_w_up, ffn2_w_down, final_norm):
    for i in range(DEPTH):
        x = x + 0.5 * swiglu(rmsnorm(x, ffn1_norm[i]), ffn1_w_gate[i], ffn1_w_up[i], ffn1_w_down[i])
        h = rmsnorm(x, mix_norm[i])
        j = i // N_MIXERS
        if i % N_MIXERS == 0:
            x = x + diff_attention(h, attn_w_qkv[j], attn_lambda_q1[j], attn_lambda_k1[j],
                                   attn_lambda_q2[j], attn_lambda_k2[j], attn_subln[j],
                                   attn_w_out[j], i)
        else:
            x = x + short_conv(h, conv_w_in[j], conv_w[j], conv_w_out[j])
        x = x + 0.5 * swiglu(rmsnorm(x, ffn2_norm[i]), ffn2_w_gate[i], ffn2_w_up[i], ffn2_w_down[i])
    return rmsnorm(x, final_norm)
```

```python
import math
from contextlib import ExitStack

import numpy as np
import concourse.bass as bass
import concourse.mybir as mybir
from concourse.bass_utils import run_bass_kernel_spmd

F32 = mybir.dt.float32
BF16 = mybir.dt.bfloat16
ALU = mybir.AluOpType
AF = mybir.ActivationFunctionType
AX = mybir.AxisListType

P = 128
D = 1024
DC = 8
FF = 2816
FC = 22
TW = 512
NCH = 8
HALF = 4096
NKT = 64
NH = 8
QCOLS = 128 + HALF
EPS = 1e-5
LAMBDA_INIT = 0.8 - 0.6 * math.exp(-0.3 * 0)
VW = 129


class Sem:
    def __init__(self, h, name):
        self.h = h
        self.v = 0
        self.name = name


class Ring:
    def __init__(self, tiles):
        self.tiles = tiles
        self.nb = len(tiles)
        self.n = 0
        self.free = {}

    def prev_free(self):
        n = self.n
        return self.free.get(n - self.nb) if n >= self.nb else None


class KB:
    def __init__(self, nc, es):
        self.nc = nc
        self.es = es
        self.E = {"pe": nc.tensor, "act": nc.scalar, "dve": nc.vector, "pool": nc.gpsimd, "sp": nc.sync}
        self.cnt = {e: self.newsem("c_" + e) for e in ("pe", "act", "dve", "pool")}
        self.waited = {}
        self.dma_sems = []

    def newsem(self, name):
        h = self.es.enter_context(self.nc.semaphore(name))
        return Sem(h, name)

    def newdsem(self, name):
        s = self.newsem(name)
        self.dma_sems.append(s)
        return s

    def sb(self, name, shape, dt):
        return self.es.enter_context(self.nc.sbuf_tensor(name, list(shape), dt))

    def ps(self, name, shape, dt=F32):
        return self.es.enter_context(self.nc.psum_tensor(name, list(shape), dt))

    def sig(self, e, ins):
        c = self.cnt[e]
        ins.then_inc(c.h, 1)
        c.v += 1
        return (c, c.v)

    def wait(self, e, tok):
        if tok is None:
            return
        sem, v = tok
        if v <= 0:
            return
        key = (e, sem.name)
        if self.waited.get(key, 0) >= v:
            return
        self.E[e].wait_ge(sem.h, v)
        self.waited[key] = v

    def dma(self, q, out, in_, sem, **kw):
        ins = self.E[q].dma_start(out=out, in_=in_, **kw)
        ins.then_inc(sem.h, 16)
        sem.v += 16
        return (sem, sem.v)

    def barrier(self):
        nc = self.nc
        for e in ("act", "dve", "pool"):
            if e == "act":
                ins = self.E[e].copy(out=self.junk[e][:, 0:1], in_=self.junk[e][:, 1:2])
            else:
                ins = self.E[e].memset(self.junk[e][:, 0:1], 0.0)
            self.sig(e, ins)
        toks = [(c, c.v) for c in self.cnt.values()] + [(s, s.v) for s in self.dma_sems]
        for e in ("pe", "act", "dve", "pool", "sp"):
            for t in toks:
                self.wait(e, t)


def build_program(debug=False, stop_after="C"):
    nc = bass.Bass("TRN2", target_bir_lowering=False)
    es = ExitStack()
    k = KB(nc, es)
    pe, act, dve, pool, sp = nc.tensor, nc.scalar, nc.vector, nc.gpsimd, nc.sync

    def din(name, shape, dt=F32):
        return nc.dram_tensor(name, list(shape), dt, kind="ExternalInput").ap()

    def dscr(name, shape, dt, dbg=False):
        kind = "ExternalOutput" if (debug and dbg) else "Internal"
        return nc.dram_tensor(name, list(shape), dt, kind=kind).ap()

    x_r0 = din("x_r0", [NCH * P, DC * TW])
    x_own = din("x_own", [NCH * P, DC * TW])
    out_d = nc.dram_tensor("out", [NCH * P, DC * TW], F32, kind="ExternalOutput").ap()
    gains_d = din("gains", [P, 7 * DC])
    convw_d = din("convw", [P, 3 * DC])
    vis_d = din("vis", [P, NKT])
    halo_d = din("halo_on", [P, 1])
    lamv_d = din("lamv", [1, 4 * 64])
    subln_d = din("subln", [1, P])
    ident_d = din("ident", [P, P])
    masks_d = din("masks", [P, 4 * TW])

    wshapes = {}
    for l in range(2):
        for f in (1, 2):
            wshapes[f"wgu{l}{f}"] = FC * P * 2 * DC * P
            wshapes[f"wd{l}{f}"] = DC * P * FC * P
    wshapes["wqk"] = 16 * P * DC * P
    wshapes["wv"] = P * DC * D
    wshapes["wo"] = P * DC * D
    wshapes["wcin"] = DC * P * 3 * DC * P
    wshapes["wcout"] = P * DC * D
    w32 = {n: din(n, [sz // 2048, 2048]) for n, sz in wshapes.items()}
    w16 = {n: dscr(n + "_b", [sz // 2048, 2048], BF16) for n, sz in wshapes.items()}

    def wview(name, line):
        t = w16[name]
        if line == 2048:
            return t
        return t.rearrange("a b -> (a b)").rearrange("(r c) -> r c", c=line)

    k_scr = dscr("k_scr", [NH * P, 2 * HALF], BF16, dbg=True)
    v_scr = dscr("v_scr", [NH * P, NKT * VW], BF16, dbg=True)
    q_scr = dscr("q_scr", [NH * P, QCOLS], BF16, dbg=True)
    ot_scr = dscr("ot_scr", [NH * P, QCOLS], BF16, dbg=True)
    xs = dscr("xs", [NCH * P, DC * TW], F32, dbg=True)
    xs_h = dscr("xs_h", [P, DC * P], F32, dbg=True)

    gains = k.sb("gains_s", [P, 7, DC], F32)
    convw = k.sb("convw_s", [P, 3, DC], F32)
    vis = k.sb("vis_s", [P, NKT], F32)
    halo_on = k.sb("halo_s", [P, 1], F32)
    lamv = k.sb("lamv_s", [P, 4, 64], F32)
    gsub = k.sb("gsub_s", [P, P], F32)
    ident = k.sb("ident_s", [P, P], BF16)
    masks = k.sb("masks_s", [P, 4, TW], BF16)
    ones_bf = k.sb("ones_bf", [P, P], BF16)
    ones81 = k.sb("ones81", [P, NH, 1], F32)
    nlam = k.sb("nlam", [P, 1], F32)
    mhalf = k.sb("mhalf", [P, 4], F32)
    lam_t = k.sb("lam_t", [P, 4], F32)
    lam_p = k.sb("lam_p", [P, 2, 64], F32)
    k.junk = {e: k.sb("junk_" + e, [P, 4], F32) for e in ("act", "dve", "pool")}

    accp = [k.ps(f"accp{i}", [P, 2, TW]) for i in range(2)]
    dnb = [k.ps(f"dnb{i}", [P, TW]) for i in range(2)]
    ssb = k.ps("ssb", [P, TW])
    tpb = k.ps("tpb", [P, 2 * TW], BF16)
    acc = Ring([accp[0][:, 0, :], accp[0][:, 1, :], accp[1][:, 0, :], accp[1][:, 1, :]])
    dnr = Ring([dnb[0][:], dnb[1][:]])

    s_const = k.newdsem("s_const")
    s_constp = k.newdsem("s_constp")
    s_castq = [k.newdsem(f"s_castq{i}") for i in range(6)]
    s_castC = k.newdsem("s_castC")
    s_x = [k.newdsem(f"s_x{i}") for i in range(2)]
    s_xst = [k.newdsem(f"s_xst{i}") for i in range(2)]
    s_wgu = [k.newdsem(f"s_wgu{i}") for i in range(3)]
    s_wd = [k.newdsem(f"s_wd{i}") for i in range(2)]
    s_res = k.newdsem("s_res")
    s_st = [k.newdsem(f"s_st{i}") for i in range(2)]
    s_vst = k.newdsem("s_vst")
    s_hd = [k.newdsem(f"s_hd{i}") for i in range(2)]
    s_ot = [k.newdsem(f"s_ot{i}") for i in range(2)]
    s_otl = k.newdsem("s_otl")
    s_out = [k.newdsem(f"s_out{i}") for i in range(2)]

    class WRing:
        def __init__(self, tiles, sems):
            self.tiles, self.sems, self.nb = tiles, sems, len(tiles)
            self.n_load = 0
            self.n_cons = 0
            self.loaded = {}
            self.rel = {}

        def load(self, src, dst_fn):
            n = self.n_load
            s = n % self.nb
            if n >= self.nb:
                k.wait("sp", self.rel[n - self.nb])
            self.loaded[n] = k.dma("sp", dst_fn(self.tiles[s]), src, self.sems[s])
            self.n_load += 1

        def take(self):
            n = self.n_cons
            self.n_cons += 1
            k.wait("pe", self.loaded[n])
            return n, self.tiles[n % self.nb]

    class NS:
        pass

    B = NS()

    def alloc_ffn(stack, tag):
        def sbt(name, shape, dt):
            return stack.enter_context(nc.sbuf_tensor(name + tag, list(shape), dt))
        B.xc = [sbt(f"xc{i}", [P, DC, TW], F32) for i in range(2)]
        B.hT = sbt("hT", [P, DC, TW], BF16)
        B.rstd = sbt("rstd", [P, TW], F32)
        B.sq = [sbt(f"sq{i}", [P, TW], BF16) for i in range(4)]
        B.aT = sbt("aT", [P, FC, TW], BF16)
        B.sg = [sbt(f"sg{i}", [P, TW], F32) for i in range(2)]
        wgu_t = [sbt(f"wgu_t{i}", [P, 3, DC, P], BF16) for i in range(3)]
        wd_t = [sbt(f"wd_t{i}", [P, FC, P], BF16) for i in range(2)]
        B.wgu_r = WRing(wgu_t, s_wgu)
        B.wd_r = WRing(wd_t, s_wd)

    st = {"hT_free": None, "aT_free": None, "rstd_free": None, "ssb_free": None, "sg_free": [None, None],
          "sq_free": [None] * 4, "sq_n": 0}

    k.dma("sp", gains[:].rearrange("p a b -> p (a b)"), gains_d, s_const)
    k.dma("sp", convw[:].rearrange("p a b -> p (a b)"), convw_d, s_const)
    k.dma("sp", vis[:], vis_d, s_const)
    k.dma("sp", halo_on[:], halo_d, s_const)
    k.dma("sp", lamv[:].rearrange("p a b -> p (a b)"), lamv_d.partition_broadcast(P), s_const)
    k.dma("sp", gsub[:], subln_d.partition_broadcast(P), s_const)
    tok_const = (s_const, s_const.v)
    k.dma("pool", ident[:], ident_d, s_constp)
    k.dma("pool", masks[:].rearrange("p a b -> p (a b)"), masks_d, s_constp)
    tok_constp = (s_constp, s_constp.v)
    orderA = ["wgu01", "wd01", "wqk", "wv"]
    orderC = ["wo", "wgu02", "wd02", "wgu11", "wd11", "wcin", "wcout", "wgu12", "wd12"]
    cast_tok = {}
    nrow = wshapes["wgu01"] // 2048
    qr = nrow // 4
    prev_cast = [tok_constp]

    def cast(dst, src, sem):
        k.wait("pool", prev_cast[0])
        t = k.dma("pool", dst, src, sem)
        prev_cast[0] = t
        return t
    for i in range(4):
        cast_tok[("wgu01", i)] = cast(w16["wgu01"][i * qr:(i + 1) * qr, :], w32["wgu01"][i * qr:(i + 1) * qr, :],
                                      s_castq[i])
    cast_tok["wd01"] = cast(w16["wd01"], w32["wd01"], s_castq[4])
    cast(w16["wqk"], w32["wqk"], s_castq[5])
    cast_tok["wqkv"] = cast(w16["wv"], w32["wv"], s_castq[5])
    castA_all = [cast_tok[("wgu01", i)] for i in range(4)] + [cast_tok["wd01"], cast_tok["wqkv"]]
    k.wait("pool", tok_constp)
    for t_ in castA_all:
        k.wait("pool", t_)
    castC_pieces = []
    for n in orderC:
        nr = wshapes[n] // 2048
        if nr > 1024:
            castC_pieces.append((w16[n][0:nr // 2, :], w32[n][0:nr // 2, :]))
            castC_pieces.append((w16[n][nr // 2:nr, :], w32[n][nr // 2:nr, :]))
        else:
            castC_pieces.append((w16[n], w32[n]))

    dve.memset(ones_bf[:], 1.0)
    dve.memset(mhalf[:], -0.5)
    k.sig("dve", dve.memset(ones81[:], 1.0))
    tok_pool_const = (k.cnt["dve"], k.cnt["dve"].v)
    k.wait("dve", tok_const)
    dve.tensor_tensor(out=lam_p[:, 0, :], in0=lamv[:, 0, :], in1=lamv[:, 1, :], op=ALU.mult)
    k.wait("dve", k.sig("dve", dve.tensor_tensor(out=lam_p[:, 1, :], in0=lamv[:, 2, :], in1=lamv[:, 3, :],
                                                 op=ALU.mult)))
    t_l = k.sig("dve", dve.reduce_sum(out=lam_t[:, 0:2], in_=lam_p[:], axis=AX.X))
    k.wait("act", t_l)
    t_l2 = k.sig("act", act.activation(out=lam_t[:, 2:4], in_=lam_t[:, 0:2], func=AF.Exp))
    k.wait("dve", t_l2)
    k.wait("dve", k.sig("dve", dve.tensor_tensor(out=nlam[:], in0=lam_t[:, 3:4], in1=lam_t[:, 2:3],
                                                 op=ALU.subtract)))
    k.wait("dve", k.sig("dve", dve.tensor_scalar(out=nlam[:], in0=nlam[:], scalar1=-LAMBDA_INIT, scalar2=None,
                                                 op0=ALU.add)))
    dve.tensor_scalar(out=gsub[:], in0=gsub[:], scalar1=1.0 - LAMBDA_INIT, scalar2=None, op0=ALU.mult)
    for e in ("pe", "act", "dve", "pool"):
        k.wait(e, tok_const)
        k.wait(e, tok_constp)
        k.wait(e, tok_pool_const)

    def stats_step(xv, W, dc, x_tok):
        s = st["sq_n"] % len(B.sq)
        st["sq_n"] += 1
        k.wait("act", x_tok)
        k.wait("act", st["sq_free"][s])
        t_sq = k.sig("act", act.activation(out=B.sq[s][:, :W], in_=xv[:, dc, :W], func=AF.Square))

        def run():
            k.wait("pe", t_sq)
            if dc == 0:
                k.wait("pe", st["ssb_free"])
            t = k.sig("pe", pe.matmul(ssb[:, :W], lhsT=ones_bf[:], rhs=B.sq[s][:, :W], start=(dc == 0),
                                      stop=(dc == DC - 1)))
            st["sq_free"][s] = t
            return t
        return run

    def norm(xv, W, gidx, x_tok, write_h=True, ss_tok=None):
        last_pe = ss_tok
        k.wait("act", x_tok)
        for dc in (range(DC) if ss_tok is None else []):
            s = st["sq_n"] % len(B.sq)
            st["sq_n"] += 1
            k.wait("act", st["sq_free"][s])
            t_sq = k.sig("act", act.activation(out=B.sq[s][:, :W], in_=xv[:, dc, :W], func=AF.Square))
            k.wait("pe", t_sq)
            if dc == 0:
                k.wait("pe", st["ssb_free"])
            last_pe = k.sig("pe", pe.matmul(ssb[:, :W], lhsT=ones_bf[:], rhs=B.sq[s][:, :W],
                                            start=(dc == 0), stop=(dc == DC - 1)))
            st["sq_free"][s] = last_pe
        k.wait("act", last_pe)
        k.wait("act", st["rstd_free"])
        t_a = k.sig("act", act.activation(out=B.rstd[:, :W], in_=ssb[:, :W], func=AF.Sqrt, scale=1.0 / D,
                                          bias=eps_t[:, 0:1]))
        st["ssb_free"] = t_a
        k.wait("dve", t_a)
        k.wait("dve", x_tok)
        t_d = k.sig("dve", dve.reciprocal(out=B.rstd[:, :W], in_=B.rstd[:, :W]))
        if not write_h:
            return [t_d] * DC
        k.wait("dve", st["hT_free"])
        toks = []
        for dc in range(DC):
            toks.append(k.sig("dve", dve.scalar_tensor_tensor(
                out=B.hT[:, dc, :W], in0=xv[:, dc, :W], scalar=gains[:, gidx, dc:dc + 1], in1=B.rstd[:, :W],
                op0=ALU.mult, op1=ALU.mult)))
        st["rstd_free"] = toks[-1]
        return toks

    def acq():
        A = acc.tiles[acc.n % acc.nb]
        k.wait("pe", acc.prev_free())
        nA = acc.n
        acc.n += 1
        return A, nA

    def ffn(xv, W, x_tok, wname_gu, wname_d, gidx, ss_tok=None, first=False):
        wg_src = wview(wname_gu, 2048)
        wd_src = wview(wname_d, FC * P)
        wg_dst = lambda t: t[:, 0:2, :, :].rearrange("p a b c -> p (a b c)")
        wd_dst = lambda t: t[:].rearrange("p a b -> p (a b)")
        npre_g = min(B.wgu_r.nb, FC)

        def cast_gate(fc):
            if first:
                k.wait("sp", cast_tok[("wgu01", min(3, (fc * P) // qr))])
                k.wait("sp", cast_tok[("wgu01", min(3, ((fc + 1) * P - 1) // qr))])
        for fc in range(npre_g):
            cast_gate(fc)
            B.wgu_r.load(wg_src[fc * P:(fc + 1) * P, :], wg_dst)
        h_toks = norm(xv, W, gidx, x_tok, ss_tok=ss_tok)
        npre_d = min(B.wd_r.nb, DC)
        state = {"last_d": None, "last_pe": None, "wd_pref": False}

        def evac(fc, G, nG, U, nU, t_pe):
            s = fc % 2
            k.wait("act", t_pe)
            k.wait("act", st["sg_free"][s])
            t_a = k.sig("act", act.activation(out=B.sg[s][:, :W], in_=G[:, :W], func=AF.Silu))
            k.wait("dve", t_a)
            if fc == 0:
                k.wait("dve", st["aT_free"])
            t_d = k.sig("dve", dve.tensor_tensor(out=B.aT[:, fc, :W], in0=B.sg[s][:, :W], in1=U[:, :W],
                                                 op=ALU.mult))
            acc.free[nG] = t_d
            acc.free[nU] = t_d
            st["sg_free"][s] = t_d
            state["last_d"] = t_d

        def after_fc(fc, n, t_pe):
            state["last_pe"] = t_pe
            B.wgu_r.rel[n] = t_pe
            if fc + npre_g < FC:
                f2 = fc + npre_g
                cast_gate(f2)
                B.wgu_r.load(wg_src[f2 * P:(f2 + 1) * P, :], wg_dst)
            if fc == (FC - 1 if first else 8):
                if first:
                    k.wait("sp", cast_tok["wd01"])
                for do in range(npre_d):
                    B.wd_r.load(wd_src[do * P:(do + 1) * P, :], wd_dst)

        n0, w0 = B.wgu_r.take()
        n1, w1 = B.wgu_r.take()
        slots = [acq() for _ in range(4)]
        wts = [w0, w0, w1, w1]
        toks = [None] * 4
        for dc in range(DC):
            k.wait("pe", h_toks[dc])
            for i in range(4):
                ins = pe.matmul(slots[i][0][:, :W], lhsT=wts[i][:, i % 2, dc, :], rhs=B.hT[:, dc, :W],
                                start=(dc == 0), stop=(dc == DC - 1))
                if dc == DC - 1 and i % 2 == 1:
                    toks[i] = k.sig("pe", ins)
        after_fc(0, n0, toks[1])
        after_fc(1, n1, toks[3])
        evac(0, slots[0][0], slots[0][1], slots[1][0], slots[1][1], toks[1])
        evac(1, slots[2][0], slots[2][1], slots[3][0], slots[3][1], toks[3])
        for fc in range(2, FC):
            n, wt = B.wgu_r.take()
            G, nG = acq()
            U, nU = acq()
            for dc in range(DC):
                pe.matmul(G[:, :W], lhsT=wt[:, 0, dc, :], rhs=B.hT[:, dc, :W], start=(dc == 0), stop=(dc == DC - 1))
            for dc in range(DC):
                ins = pe.matmul(U[:, :W], lhsT=wt[:, 1, dc, :], rhs=B.hT[:, dc, :W], start=(dc == 0),
                                stop=(dc == DC - 1))
            t_pe = k.sig("pe", ins)
            after_fc(fc, n, t_pe)
            evac(fc, G, nG, U, nU, t_pe)
        st["hT_free"] = state["last_pe"]
        last_d = state["last_d"]
        x_done = None
        pending = None
        for do in range(DC):
            n, wt = B.wd_r.take()
            k.wait("pe", last_d)
            Y = dnr.tiles[dnr.n % dnr.nb]
            k.wait("pe", dnr.prev_free()); nY = dnr.n; dnr.n += 1
            for fc in range(FC):
                ins = pe.matmul(Y[:, :W], lhsT=wt[:, fc, :], rhs=B.aT[:, fc, :W], start=(fc == 0), stop=(fc == FC - 1))
            t_pe = k.sig("pe", ins)
            B.wd_r.rel[n] = t_pe
            if pending is not None:
                pending()
            if do + npre_d < DC:
                d2 = do + npre_d
                B.wd_r.load(wd_src[d2 * P:(d2 + 1) * P, :], wd_dst)
            k.wait("dve", t_pe)
            x_done = k.sig("dve", dve.scalar_tensor_tensor(out=xv[:, do, :W], in0=Y[:, :W], scalar=0.5,
                                                           in1=xv[:, do, :W], op0=ALU.mult, op1=ALU.add))
            dnr.free[nY] = x_done
            st["aT_free"] = t_pe
            pending = stats_step(xv, W, do, x_done)
        ss_next = pending()
        return x_done, ss_next

    eps_t = k.sb("eps_t", [P, 1], F32)
    k.sig("dve", dve.memset(eps_t[:], EPS))
    k.wait("act", (k.cnt["dve"], k.cnt["dve"].v))

    phaseA = ExitStack()
    alloc_ffn(phaseA, "A")
    wqk_s = phaseA.enter_context(nc.sbuf_tensor("wqk_s", [P, 16, DC * P], BF16))
    wv_s = phaseA.enter_context(nc.sbuf_tensor("wv_s", [P, DC, D], BF16))
    kst = [phaseA.enter_context(nc.sbuf_tensor(f"kst{i}", [P, TW], BF16)) for i in range(2)]
    vst = phaseA.enter_context(nc.sbuf_tensor("vst", [P, NH, 4, VW], BF16))

    resA = {}

    def load_resA():
        k.wait("sp", cast_tok["wqkv"])
        k.dma("sp", wqk_s[:], wview("wqk", DC * P).rearrange("(kc p) x -> p kc x", p=P), s_res)
        k.dma("sp", wv_s[:].rearrange("p a b -> p (a b)"), wview("wv", DC * D), s_res)
        resA["tok"] = (s_res, s_res.v)

    def x_src(g):
        src = x_r0 if g < NCH else x_own
        c = g % NCH
        return src[c * P:(c + 1) * P, :]

    x_ld = {}
    x_slot_free = [[], []]

    def issue_xload(g):
        s = g % 2
        for t in x_slot_free[s]:
            k.wait("sp", t)
        x_slot_free[s] = []
        x_ld[g] = k.dma("sp", B.xc[s][:].rearrange("p a b -> p (a b)"), x_src(g), s_x[s])

    NG = 2 * NCH
    issue_xload(0)
    issue_xload(1)
    st_i = 0
    st_free = [None, None]
    vst_free = None
    for g in range(NG):
        s = g % 2
        xv = B.xc[s]
        own = g >= NCH
        c = g % NCH
        tok0 = g * TW
        x_done, ss_n = ffn(xv, TW, x_ld[g], "wgu01", "wd01", 0, first=(g == 0))
        if g == 0:
            load_resA()
        tok_resA = resA["tok"]
        if own:
            k.wait("act", x_done)
            t_xs = k.dma("act", xs[c * P:(c + 1) * P, :], xv[:].rearrange("p a b -> p (a b)"), s_xst[s])
            x_slot_free[s].append(t_xs)
        if g == NCH - 1:
            k.wait("act", x_done)
            t_xs = k.dma("act", xs_h.rearrange("p (a b) -> p a b", a=DC), xv[:, :, TW - P:TW], s_xst[s])
            x_slot_free[s].append(t_xs)
        h_toks = norm(xv, TW, 1, x_done, ss_tok=ss_n)
        t_h = h_toks[-1]
        x_slot_free[s].append(t_h)
        k.wait("pe", tok_resA)
        first_group = True
        jobs = [("k", kc) for kc in range(NH)]
        if own:
            jobs += [("q", kc) for kc in range(NH)]
        elif g == NCH - 1:
            jobs += [("qh", kc) for kc in range(NH)]
        last_pe = None
        for kind, kc in jobs:
            A = acc.tiles[acc.n % acc.nb]
            k.wait("pe", acc.prev_free()); nA = acc.n; acc.n += 1
            widx = (8 + kc) if kind == "k" else kc
            if kind == "qh":
                cs, W = TW - P, P
            else:
                cs, W = 0, TW
            for dc in range(DC):
                if first_group:
                    k.wait("pe", h_toks[dc])
                ins = pe.matmul(A[:, :W], lhsT=wqk_s[:, widx, dc * P:(dc + 1) * P], rhs=B.hT[:, dc, cs:cs + W],
                                start=(dc == 0), stop=(dc == DC - 1))
            first_group = False
            t_pe = k.sig("pe", ins)
            last_pe = t_pe
            ss_ = st_i % 2
            st_i += 1
            k.wait("act", t_pe)
            k.wait("act", st_free[ss_])
            t_a = k.sig("act", act.copy(out=kst[ss_][:, :W], in_=A[:, :W]))
            acc.free[nA] = t_a
            k.wait("act", t_a)
            if kind == "k":
                dst = k_scr[kc * P:(kc + 1) * P, tok0:tok0 + TW]
            elif kind == "q":
                dst = q_scr[kc * P:(kc + 1) * P, P + c * TW:P + (c + 1) * TW]
            else:
                dst = q_scr[kc * P:(kc + 1) * P, 0:P]
            st_free[ss_] = k.dma("act", dst, kst[ss_][:, :W], s_st[ss_])
        for tt in range(4):
            j = g * 4 + tt
            for half in range(2):
                A = acc.tiles[acc.n % acc.nb]
                k.wait("pe", acc.prev_free()); nA = acc.n; acc.n += 1
                for dc in range(DC):
                    ins = pe.matmul(A[:, :TW], lhsT=B.hT[:, dc, tt * P:(tt + 1) * P],
                                    rhs=wv_s[:, dc, half * TW:(half + 1) * TW], start=(dc == 0), stop=(dc == DC - 1))
                t_pe = k.sig("pe", ins)
                last_pe = t_pe
                k.wait("act", t_pe)
                if tt == 0 and half == 0:
                    k.wait("act", vst_free)
                    k.wait("dve", vst_free)
                t_a = k.sig("act", act.mul(out=vst[:, half * 4:(half + 1) * 4, tt, 0:P],
                                           in_=A[:, :TW].rearrange("p (h e) -> p h e", h=4), mul=vis[:, j:j + 1]))
                acc.free[nA] = t_a
            t_v1 = k.sig("dve", dve.tensor_scalar(out=vst[:, :, tt, P:VW], in0=ones81[:], scalar1=vis[:, j:j + 1],
                                                  scalar2=None, op0=ALU.mult))
        st["hT_free"] = last_pe
        k.wait("act", t_a)
        k.wait("act", t_v1)
        vst_free = k.dma("act", v_scr.rearrange("(h p) c -> p h c", p=P)[:, :, g * 4 * VW:(g + 1) * 4 * VW],
                         vst[:].rearrange("p h t e -> p h (t e)"), s_vst)
        if g + 2 < NG:
            issue_xload(g + 2)
    k.barrier()
    phaseA.close()

    if stop_after == "A":
        es_close(k, es)
        return nc

    phaseB = ExitStack()
    kt = [phaseB.enter_context(nc.sbuf_tensor(f"kt{i}", [P, 2 * HALF], BF16)) for i in range(2)]
    vt = [phaseB.enter_context(nc.sbuf_tensor(f"vt{i}", [P, NKT, VW], BF16)) for i in range(2)]
    qt = [phaseB.enter_context(nc.sbuf_tensor(f"qt{i}", [P, QCOLS], BF16)) for i in range(2)]
    oth = [phaseB.enter_context(nc.sbuf_tensor(f"oth{i}", [P, QCOLS], BF16)) for i in range(2)]
    Et = [phaseB.enter_context(nc.sbuf_tensor(f"Et{i}", [P, 2, TW], BF16)) for i in range(4)]
    on_st = phaseB.enter_context(nc.sbuf_tensor("on_st", [P, 4, P], BF16))
    o_tmp = phaseB.enter_context(nc.sbuf_tensor("o_tmp", [P, 4, P], F32))
    a_tmp = phaseB.enter_context(nc.sbuf_tensor("a_tmp", [P, 4, P], F32))
    j_tmp = phaseB.enter_context(nc.sbuf_tensor("j_tmp", [P, 4, P], F32))
    rs8 = phaseB.enter_context(nc.sbuf_tensor("rs8", [P, 8], F32))
    o_sb = phaseB.enter_context(nc.sbuf_tensor("o_sb", [P, 3, 3 * VW], F32))
    rr8 = phaseB.enter_context(nc.sbuf_tensor("rr8", [P, 8], F32))
    nl4 = phaseB.enter_context(nc.sbuf_tensor("nl4", [P, 4], F32))
    ssq = phaseB.enter_context(nc.sbuf_tensor("ssq", [P, 4], F32))
    sd = phaseB.enter_context(nc.sbuf_tensor("sd", [P, 4], F32))

    obanks = [dnb[0], dnb[1], ssb]

    def oacc(c, t):
        i = c * 4 + t
        return obanks[i // 3][:, (i % 3) * VW:(i % 3 + 1) * VW], i // 3

    sring = Ring([accp[0], accp[1]])
    ering = Ring(Et)
    hd_free = [None, None]
    oth_free = [None, None]
    of_free = None
    onst_free = None
    otmp_free = None
    tp_free = None
    ssq_free = None
    sd_free = None

    qlist = [("h", 0)] + [("o", i) for i in range(NCH)]
    att_step = [0]
    deferred = []
    t_c = None
    for h in range(NH):
        hs = h % 2
        k.wait("sp", hd_free[hs])
        k.dma("sp", kt[hs][:], k_scr[h * P:(h + 1) * P, :], s_hd[hs])
        k.dma("sp", vt[hs][:].rearrange("p a b -> p (a b)"), v_scr[h * P:(h + 1) * P, :], s_hd[hs])
        t_ld = k.dma("sp", qt[hs][:], q_scr[h * P:(h + 1) * P, :], s_hd[hs])
        KT, VT, QT, OTH = kt[hs], vt[hs], qt[hs], oth[hs]
        k.wait("dve", oth_free[hs])
        last_head_pe = None
        pendq = []

        def emit_av(pd):
            nonlocal last_head_pe
            Etile, nE, tok_e, j, first, last, tmin, ctx = pd
            nt_ = ctx["nt"]
            bs_ = ctx["bs"]
            k.wait("pe", tok_e)
            if first:
                k.wait("pe", of_free)
            ins = None
            for c in range(2):
                for t in range(tmin, nt_):
                    o_ap, b = oacc(c, t)
                    stflag = False
                    if first and not bs_[b]:
                        stflag = True
                        bs_[b] = True
                    ins = pe.matmul(o_ap, lhsT=Etile[:, c, t * P:(t + 1) * P], rhs=VT[:, j, :], start=stflag,
                                    stop=last, skip_group_check=True)
            t_av = k.sig("pe", ins)
            ering.free[nE] = t_av
            last_head_pe = t_av
            if last:
                ctx["fin"](t_av)
            return t_av

        for kind, qi in qlist:
            if kind == "h":
                W, qoff = P, 0
                tiles = [(j, None) for j in range(31)] + [(31, 0)]
            else:
                W, qoff = TW, P + qi * TW
                tiles = [(j, None) for j in range(32 + 4 * qi)] + [(32 + 4 * qi + m, m) for m in range(4)]
            nt = W // P
            ctx = {"nt": nt, "bs": [False, False, False]}

            def finalize(t_av_last, nt=nt, W=W, qoff=qoff, OTH=OTH):
                nonlocal of_free, otmp_free, ssq_free, sd_free
                k.wait("dve", t_av_last)
                k.wait("dve", otmp_free)
                k.wait("dve", ssq_free)
                if nt == 4:
                    dve.tensor_copy(out=o_sb[:, 0, :], in_=obanks[0][:, 0:3 * VW])
                    dve.tensor_copy(out=o_sb[:, 1, :], in_=obanks[1][:, 0:3 * VW])
                    ins = dve.tensor_copy(out=o_sb[:, 2, 0:2 * VW], in_=obanks[2][:, 0:2 * VW])
                else:
                    dve.tensor_copy(out=o_sb[:, 0, 0:VW], in_=obanks[0][:, 0:VW])
                    ins = dve.tensor_copy(out=o_sb[:, 1, VW:2 * VW], in_=obanks[1][:, VW:2 * VW])
                of_free = k.sig("dve", ins)
                k.wait("dve", of_free)

                def oacc_s(c, t):
                    i = c * 4 + t
                    return o_sb[:, i // 3, (i % 3) * VW:(i % 3 + 1) * VW]
                for t in range(nt):
                    o1 = oacc_s(0, t)
                    o2 = oacc_s(1, t)
                    dve.tensor_scalar(out=rs8[:, t:t + 1], in0=o1[:, P:VW], scalar1=1e-30, scalar2=None, op0=ALU.max)
                    ins = dve.tensor_scalar(out=rs8[:, 4 + t:5 + t], in0=o2[:, P:VW], scalar1=1e-30, scalar2=None,
                                            op0=ALU.max)
                k.wait("dve", k.sig("dve", ins))
                if nt == 4:
                    ins = dve.reciprocal(out=rr8[:, 0:8], in_=rs8[:, 0:8])
                else:
                    dve.reciprocal(out=rr8[:, 0:1], in_=rs8[:, 0:1])
                    ins = dve.reciprocal(out=rr8[:, 4:5], in_=rs8[:, 4:5])
                k.wait("dve", k.sig("dve", ins))
                ins = dve.tensor_scalar(out=nl4[:, 0:nt], in0=rr8[:, 4:4 + nt], scalar1=nlam[:, 0:1], scalar2=None,
                                        op0=ALU.mult)
                k.wait("dve", k.sig("dve", ins))
                for t in range(nt):
                    o1 = oacc_s(0, t)
                    dve.tensor_scalar(out=a_tmp[:, t, :], in0=o1[:, 0:P], scalar1=rr8[:, t:t + 1], scalar2=None,
                                      op0=ALU.mult)
                for t in range(nt):
                    o2 = oacc_s(1, t)
                    ins = dve.scalar_tensor_tensor(out=o_tmp[:, t, :], in0=o2[:, 0:P], scalar=nl4[:, t:t + 1],
                                                   in1=a_tmp[:, t, :], op0=ALU.mult, op1=ALU.add)
                t_f1 = k.sig("dve", ins)
                k.wait("dve", t_f1)
                dve.tensor_tensor(out=j_tmp[:, :nt, :], in0=o_tmp[:, :nt, :], in1=o_tmp[:, :nt, :], op=ALU.mult)
                k.wait("dve", k.sig("dve", dve.reduce_sum(out=ssq[:, :nt], in_=j_tmp[:, :nt, :], axis=AX.X)))
                t_ss = k.sig("dve", dve.tensor_scalar(out=ssq[:, :nt], in0=ssq[:, :nt], scalar1=1.0 / P, scalar2=EPS,
                                                      op0=ALU.mult, op1=ALU.add))
                k.wait("pool", t_ss)
                k.wait("pool", sd_free)
                t_sd = k.sig("pool", pool.tensor_tensor(out=sd[:, :nt], in0=ssq[:, :nt], in1=mhalf[:, :nt], op=ALU.pow))
                att_step[0] += 1
                if castC_pieces and att_step[0] % 4 == 1:
                    dst_, src_ = castC_pieces.pop(0)
                    cast(dst_, src_, s_castC)
                ssq_free = t_sd
                k.wait("dve", t_sd)
                k.wait("dve", onst_free)
                for t in range(nt):
                    ins = dve.scalar_tensor_tensor(out=on_st[:, t, :], in0=o_tmp[:, t, :], scalar=sd[:, t:t + 1],
                                                   in1=gsub[:], op0=ALU.mult, op1=ALU.mult)
                t_f2 = k.sig("dve", ins)
                otmp_free = t_f2
                sd_free = t_f2

                def make_deferred(t_f2=t_f2, nt=nt, OTH=OTH, qoff=qoff, W=W):
                    def run():
                        nonlocal last_head_pe, onst_free, tp_free, t_c
                        k.wait("pe", t_f2)
                        k.wait("pe", tp_free)
                        for t in range(nt):
                            ins = pe.transpose(out=tpb[:, t * P:(t + 1) * P], in_=on_st[:, t, :], identity=ident[:])
                        t_t = k.sig("pe", ins)
                        last_head_pe = t_t
                        onst_free = t_t
                        k.wait("dve", t_t)
                        t_c = k.sig("dve", dve.tensor_copy(out=OTH[:, qoff:qoff + W], in_=tpb[:, :W]))
                        tp_free = t_c
                    return run
                deferred.append(make_deferred())
            ctx["fin"] = finalize
            for idx, (j, m) in enumerate(tiles):
                Sb = sring.tiles[sring.n % sring.nb]
                k.wait("pe", sring.prev_free()); nS = sring.n; sring.n += 1
                k.wait("pe", t_ld)
                c0 = P * m if m is not None else 0
                pe.matmul(Sb[:, 0, c0:W], lhsT=KT[0:64, j * P:(j + 1) * P], rhs=QT[0:64, qoff + c0:qoff + W],
                          start=True, stop=True)
                t_s = k.sig("pe", pe.matmul(Sb[:, 1, c0:W], lhsT=KT[64:128, j * P:(j + 1) * P],
                                            rhs=QT[64:128, qoff + c0:qoff + W], start=True, stop=True))
                Etile = ering.tiles[ering.n % ering.nb]
                k.wait("act", t_s)
                k.wait("act", ering.prev_free()); nE = ering.n; ering.n += 1
                t_e = k.sig("act", act.activation(out=Etile[:, :, c0:W], in_=Sb[:, :, c0:W], func=AF.Exp,
                                                  scale=0.125))
                sring.free[nS] = t_e
                if m is not None:
                    k.wait("dve", t_e)
                    dve.tensor_tensor(out=Etile[:, 0, c0:W], in0=Etile[:, 0, c0:W], in1=masks[:, m, c0:W],
                                      op=ALU.mult)
                    t_e = k.sig("dve", dve.tensor_tensor(out=Etile[:, 1, c0:W], in0=Etile[:, 1, c0:W],
                                                          in1=masks[:, m, c0:W], op=ALU.mult))
                pendq.append((Etile, nE, t_e, j, idx == 0, idx == len(tiles) - 1, (m or 0), ctx))
                if len(pendq) > 2:
                    emit_av(pendq.pop(0))
                if idx == 17:
                    while deferred:
                        deferred.pop(0)()
        while pendq:
            emit_av(pendq.pop(0))
        hd_free[hs] = last_head_pe

        def make_store(h=h, hs=hs, OTH=OTH):
            def run():
                k.wait("pool", t_c)
                oth_free[hs] = k.dma("pool", ot_scr[h * P:(h + 1) * P, :], OTH[:], s_ot[hs])
            return run
        deferred.append(make_store())
    while deferred:
        deferred.pop(0)()
    while castC_pieces:
        dst_, src_ = castC_pieces.pop(0)
        cast(dst_, src_, s_castC)
    tok_castC = (s_castC, s_castC.v)
    k.barrier()
    phaseB.close()
    if stop_after == "B":
        es_close(k, es)
        return nc

    phaseC = ExitStack()
    alloc_ffn(phaseC, "C")
    wo_s = phaseC.enter_context(nc.sbuf_tensor("wo_s", [P, DC, D], BF16))
    wco_s = phaseC.enter_context(nc.sbuf_tensor("wco_s", [P, DC, D], BF16))
    otc = [phaseC.enter_context(nc.sbuf_tensor(f"otc{i}", [P, NH, TW], BF16)) for i in range(2)]
    ubuf = phaseC.enter_context(nc.sbuf_tensor("ubuf", [P, DC, TW + 2], F32))
    zT = phaseC.enter_context(nc.sbuf_tensor("zT", [P, DC, TW], BF16))
    gcs = [phaseC.enter_context(nc.sbuf_tensor(f"gcs{i}", [P, TW], F32)) for i in range(2)]
    ytmp = phaseC.enter_context(nc.sbuf_tensor("ytmp", [P, TW], F32))

    k.wait("sp", tok_castC)
    k.dma("sp", wo_s[:].rearrange("p a b -> p (a b)"), wview("wo", DC * D), s_res)
    k.dma("sp", wco_s[:].rearrange("p a b -> p (a b)"), wview("wcout", DC * D), s_res)
    tok_resC = (s_res, s_res.v)
    k.sig("dve", dve.memset(ubuf[:, :, 0:2], 0.0))

    clist = [("h", 0)] + [("o", i) for i in range(NCH)]
    xslot_free = [[], []]
    cl_ld = {}

    def issue_cload(ci):
        kind, i = clist[ci]
        s = ci % 2
        for t in xslot_free[s]:
            k.wait("sp", t)
        xslot_free[s] = []
        if kind == "h":
            k.dma("sp", B.xc[s][:, :, 0:P], xs_h.rearrange("p (a b) -> p a b", a=DC), s_x[s])
            t2 = k.dma("sp", otc[s][:, :, 0:P], ot_scr.rearrange("(h p) c -> p h c", p=P)[:, :, 0:P], s_x[s])
        else:
            k.dma("sp", B.xc[s][:].rearrange("p a b -> p (a b)"), xs[i * P:(i + 1) * P, :], s_x[s])
            t2 = k.dma("sp", otc[s][:], ot_scr.rearrange("(h p) c -> p h c", p=P)[:, :, P + i * TW:P + (i + 1) * TW],
                       s_x[s])
        cl_ld[ci] = t2

    issue_cload(0)
    issue_cload(1)
    gcs_free = [None, None]
    zT_free = None
    ytmp_free = None
    for ci, (kind, i) in enumerate(clist):
        s = ci % 2
        W = P if kind == "h" else TW
        xv = B.xc[s]
        OT = otc[s]
        k.wait("pe", tok_resC)
        k.wait("pe", cl_ld[ci])
        k.wait("dve", cl_ld[ci])
        x_done = None
        pending = None
        for do in range(DC):
            A = acc.tiles[acc.n % acc.nb]
            k.wait("pe", acc.prev_free()); nA = acc.n; acc.n += 1
            for hc in range(NH):
                ins = pe.matmul(A[:, :W], lhsT=wo_s[:, hc, do * P:(do + 1) * P], rhs=OT[:, hc, :W], start=(hc == 0),
                                stop=(hc == NH - 1))
            t_pe = k.sig("pe", ins)
            if pending is not None:
                pending()
            k.wait("dve", t_pe)
            x_done = k.sig("dve", dve.tensor_tensor(out=xv[:, do, :W], in0=A[:, :W], in1=xv[:, do, :W], op=ALU.add))
            acc.free[nA] = x_done
            pending = stats_step(xv, W, do, x_done)
        ss_n = pending()
        xslot_free[s].append(t_pe)
        x_done, ss_n = ffn(xv, W, x_done, "wgu02", "wd02", 2, ss_tok=ss_n)
        x_done, ss_n = ffn(xv, W, x_done, "wgu11", "wd11", 3, ss_tok=ss_n)
        wc_src = wview("wcin", 3 * DC * P)
        npre = min(B.wgu_r.nb, DC)
        if kind == "h":
            wc_sel = lambda d_: wc_src[d_ * P:(d_ + 1) * P, DC * P:3 * DC * P]
            wc_dst = lambda t: t[:, 1:3, :, :].rearrange("p a b c -> p (a b c)")
        else:
            wc_sel = lambda d_: wc_src[d_ * P:(d_ + 1) * P, :]
            wc_dst = lambda t: t[:].rearrange("p a b c -> p (a b c)")
        for dc in range(npre):
            B.wgu_r.load(wc_sel(dc), wc_dst)
        h_toks = norm(xv, W, 4, x_done, ss_tok=ss_n)
        t_h = h_toks[-1]
        last_pe = None
        t_z = None
        for dc in range(DC):
            n, wt = B.wgu_r.take()
            k.wait("pe", t_h)
            accs = []
            nsub = 2 if kind == "h" else 3
            for si in ([1, 2] if kind == "h" else [1, 2, 0]):
                A = acc.tiles[acc.n % acc.nb]
                k.wait("pe", acc.prev_free()); nA = acc.n; acc.n += 1
                for kc in range(DC):
                    ins = pe.matmul(A[:, :W], lhsT=wt[:, si, kc, :], rhs=B.hT[:, kc, :W], start=(kc == 0),
                                    stop=(kc == DC - 1))
                accs.append((A, nA, k.sig("pe", ins)))
            last_pe = accs[-1][2]
            B.wgu_r.rel[n] = last_pe
            if dc + npre < DC:
                d2 = dc + npre
                B.wgu_r.load(wc_sel(d2), wc_dst)
            gs = dc % 2
            (GCa, nGC, tGC), (UUa, nUU, tUU) = accs[0], accs[1]
            k.wait("act", tGC)
            k.wait("act", gcs_free[gs])
            t_a = k.sig("act", act.copy(out=gcs[gs][:, :W], in_=GCa[:, :W]))
            acc.free[nGC] = t_a
            k.wait("dve", t_a)
            k.wait("dve", tUU)
            t_u = k.sig("dve", dve.tensor_tensor(out=ubuf[:, dc, 2:2 + W], in0=gcs[gs][:, :W], in1=UUa[:, :W],
                                                 op=ALU.mult))
            acc.free[nUU] = t_u
            gcs_free[gs] = t_u
            if kind == "h":
                k.wait("dve", t_u)
                t_u = k.sig("dve", dve.tensor_scalar(out=ubuf[:, dc, 0:2], in0=ubuf[:, dc, W:W + 2],
                                                     scalar1=halo_on[:, 0:1], scalar2=None, op0=ALU.mult))
                continue
            GBa, nGB, tGB = accs[2]
            dve.tensor_scalar(out=ytmp[:, :W], in0=ubuf[:, dc, 2:2 + W], scalar1=convw[:, 2, dc:dc + 1], scalar2=None,
                              op0=ALU.mult)
            dve.scalar_tensor_tensor(out=ytmp[:, :W], in0=ubuf[:, dc, 1:1 + W], scalar=convw[:, 1, dc:dc + 1],
                                     in1=ytmp[:, :W], op0=ALU.mult, op1=ALU.add)
            dve.scalar_tensor_tensor(out=ytmp[:, :W], in0=ubuf[:, dc, 0:W], scalar=convw[:, 0, dc:dc + 1],
                                     in1=ytmp[:, :W], op0=ALU.mult, op1=ALU.add)
            dve.tensor_copy(out=ubuf[:, dc, 0:2], in_=ubuf[:, dc, W:W + 2])
            k.wait("dve", tGB)
            if dc == 0:
                k.wait("dve", zT_free)
            t_z = k.sig("dve", dve.tensor_tensor(out=zT[:, dc, :W], in0=ytmp[:, :W], in1=GBa[:, :W], op=ALU.mult))
            acc.free[nGB] = t_z
        st["hT_free"] = last_pe
        if kind == "h":
            xslot_free[s].append(t_h)
            if ci + 2 < len(clist):
                issue_cload(ci + 2)
            continue
        k.wait("pe", t_z)
        pending = None
        for do in range(DC):
            A = acc.tiles[acc.n % acc.nb]
            k.wait("pe", acc.prev_free()); nA = acc.n; acc.n += 1
            for kc in range(DC):
                ins = pe.matmul(A[:, :W], lhsT=wco_s[:, kc, do * P:(do + 1) * P], rhs=zT[:, kc, :W], start=(kc == 0),
                                stop=(kc == DC - 1))
            t_pe = k.sig("pe", ins)
            if pending is not None:
                pending()
            k.wait("dve", t_pe)
            x_done = k.sig("dve", dve.tensor_tensor(out=xv[:, do, :W], in0=A[:, :W], in1=xv[:, do, :W], op=ALU.add))
            acc.free[nA] = x_done
            pending = stats_step(xv, W, do, x_done)
        ss_n = pending()
        zT_free = t_pe
        x_done, ss_n = ffn(xv, W, x_done, "wgu12", "wd12", 5, ss_tok=ss_n)
        t_r = norm(xv, W, 6, x_done, write_h=False, ss_tok=ss_n)[-1]
        for dc in range(DC):
            ins = dve.scalar_tensor_tensor(out=xv[:, dc, :W], in0=xv[:, dc, :W], scalar=gains[:, 6, dc:dc + 1],
                                           in1=B.rstd[:, :W], op0=ALU.mult, op1=ALU.mult)
        t_o = k.sig("dve", ins)
        st["rstd_free"] = t_o
        k.wait("act", t_o)
        t_st = k.dma("act", out_d[i * P:(i + 1) * P, :], xv[:].rearrange("p a b -> p (a b)"), s_out[s])
        xslot_free[s].append(t_st)
        if ci + 2 < len(clist):
            issue_cload(ci + 2)
    k.barrier()
    phaseC.close()
    es_close(k, es)
    return nc


def es_close(k, es):
    toks = [(s, s.v) for s in k.dma_sems]
    for t in toks:
        k.wait("sp", t)
        k.wait("act", t)
    es.close()


def _chunks_fm(xh):
    a = xh.reshape(NCH, TW, DC, P).transpose(0, 3, 2, 1)
    return np.ascontiguousarray(a).reshape(NCH * P, DC * TW)


def _unchunks_fm(o):
    a = o.reshape(NCH, P, DC, TW).transpose(0, 3, 2, 1)
    return np.ascontiguousarray(a).reshape(HALF, D)


def _vecfm(g):
    return np.ascontiguousarray(g.reshape(DC, P).T)


def _flat(a):
    return np.ascontiguousarray(a, dtype=np.float32).reshape(-1, 2048)


def _prep_weights(inp):
    w = {}
    for l in range(2):
        for f in (1, 2):
            wg = inp[f"ffn{f}_w_gate"][l].reshape(DC, P, FC, P).transpose(2, 1, 0, 3)
            wu = inp[f"ffn{f}_w_up"][l].reshape(DC, P, FC, P).transpose(2, 1, 0, 3)
            w[f"wgu{l}{f}"] = _flat(np.stack([wg, wu], axis=2))
            w[f"wd{l}{f}"] = _flat(inp[f"ffn{f}_w_down"][l].reshape(FC, P, DC, P).transpose(2, 1, 0, 3))
    wqkv = inp["attn_w_qkv"][0]
    w["wqk"] = _flat(wqkv[:, :2048].reshape(DC, P, 16, P).transpose(2, 1, 0, 3))
    w["wv"] = _flat(wqkv[:, 2048:].reshape(DC, P, D).transpose(1, 0, 2))
    w["wo"] = _flat(inp["attn_w_out"][0].reshape(DC, P, D).transpose(1, 0, 2))
    w["wcin"] = _flat(inp["conv_w_in"][0].reshape(DC, P, 3, DC, P).transpose(3, 1, 2, 0, 4))
    w["wcout"] = _flat(inp["conv_w_out"][0].reshape(DC, P, D).transpose(1, 0, 2))
    return w


def _consts():
    ident = np.eye(P, dtype=np.float32)
    p = np.arange(P)[:, None, None]
    m = np.arange(4)[None, :, None]
    f = np.arange(TW)[None, None, :]
    masks = (f - p - P * m >= 0).astype(np.float32).reshape(P, 4 * TW)
    return ident, masks


def make_in_maps(inp):
    inp = {k_: np.asarray(v, dtype=np.float32) for k_, v in inp.items()}
    w = _prep_weights(inp)
    ident, masks = _consts()
    gains = np.stack([_vecfm(inp["ffn1_norm"][0]), _vecfm(inp["mix_norm"][0]), _vecfm(inp["ffn2_norm"][0]),
                      _vecfm(inp["ffn1_norm"][1]), _vecfm(inp["mix_norm"][1]), _vecfm(inp["ffn2_norm"][1]),
                      _vecfm(inp["final_norm"])], axis=1).reshape(P, 7 * DC)
    convw = np.stack([_vecfm(inp["conv_w"][0][j]) for j in range(3)], axis=1).reshape(P, 3 * DC)
    lamv = np.concatenate([inp["attn_lambda_q1"][0], inp["attn_lambda_k1"][0], inp["attn_lambda_q2"][0],
                           inp["attn_lambda_k2"][0]]).reshape(1, 256)
    subln = inp["attn_subln"][0].reshape(1, P)
    x = inp["x"]
    maps = []
    for core in range(8):
        b, r = core // 2, core % 2
        own = x[b, r * HALF:(r + 1) * HALF]
        r0 = x[b, 0:HALF]
        vis = np.ones((P, NKT), np.float32)
        if r == 0:
            vis[:, :32] = 0.0
        m = {"x_r0": _chunks_fm(r0), "x_own": _chunks_fm(own), "gains": np.ascontiguousarray(gains),
             "convw": np.ascontiguousarray(convw), "vis": vis, "halo_on": np.full((P, 1), float(r), np.float32),
             "lamv": np.ascontiguousarray(lamv), "subln": np.ascontiguousarray(subln), "ident": ident,
             "masks": masks}
        m.update(w)
        maps.append(m)
    return maps


_NC_CACHE = {}


def kernel(**inputs):
    if "nc" not in _NC_CACHE:
        _NC_CACHE["nc"] = build_program()
    nc = _NC_CACHE["nc"]
    maps = make_in_maps(inputs)
    res = run_bass_kernel_spmd(nc, maps, core_ids=list(range(8)))
    out = np.empty((4, 2 * HALF, D), np.float32)
    for core in range(8):
        b, r = core // 2, core % 2
        out[b, r * HALF:(r + 1) * HALF] = _unchunks_fm(np.asarray(res.results[core]["out"]))
    return out
```

```python
import math
from contextlib import ExitStack

import numpy as np
import concourse.bass as bass
import concourse.mybir as mybir
from concourse.bass_utils import run_bass_kernel_spmd

F32 = mybir.dt.float32
BF16 = mybir.dt.bfloat16
ALU = mybir.AluOpType
AF = mybir.ActivationFunctionType
AX = mybir.AxisListType

P = 128
D = 1024
DC = 8
FF = 2816
FC = 22
TW = 512
NCH = 8
HALF = 4096
NKT = 64
NH = 8
QCOLS = 128 + HALF
EPS = 1e-5
LAMBDA_INIT = 0.8 - 0.6 * math.exp(-0.3 * 0)
VW = 129


class Sem:
    def __init__(self, h, name):
        self.h = h
        self.v = 0
        self.name = name


class Ring:
    def __init__(self, tiles):
        self.tiles = tiles
        self.nb = len(tiles)
        self.n = 0
        self.free = {}

    def prev_free(self):
        n = self.n
        return self.free.get(n - self.nb) if n >= self.nb else None


class KB:
    def __init__(self, nc, es):
        self.nc = nc
        self.es = es
        self.E = {"pe": nc.tensor, "act": nc.scalar, "dve": nc.vector, "pool": nc.gpsimd, "sp": nc.sync}
        self.cnt = {e: self.newsem("c_" + e) for e in ("pe", "act", "dve", "pool")}
        self.waited = {}
        self.dma_sems = []

    def newsem(self, name):
        h = self.es.enter_context(self.nc.semaphore(name))
        return Sem(h, name)

    def newdsem(self, name):
        s = self.newsem(name)
        self.dma_sems.append(s)
        return s

    def sb(self, name, shape, dt):
        return self.es.enter_context(self.nc.sbuf_tensor(name, list(shape), dt))

    def ps(self, name, shape, dt=F32):
        return self.es.enter_context(self.nc.psum_tensor(name, list(shape), dt))

    def sig(self, e, ins):
        c = self.cnt[e]
        ins.then_inc(c.h, 1)
        c.v += 1
        return (c, c.v)

    def wait(self, e, tok):
        if tok is None:
            return
        sem, v = tok
        if v <= 0:
            return
        key = (e, sem.name)
        if self.waited.get(key, 0) >= v:
            return
        self.E[e].wait_ge(sem.h, v)
        self.waited[key] = v

    def dma(self, q, out, in_, sem, **kw):
        ins = self.E[q].dma_start(out=out, in_=in_, **kw)
        ins.then_inc(sem.h, 16)
        sem.v += 16
        return (sem, sem.v)

    def barrier(self):
        nc = self.nc
        for e in ("act", "dve", "pool"):
            if e == "act":
                ins = self.E[e].copy(out=self.junk[e][:, 0:1], in_=self.junk[e][:, 1:2])
            else:
                ins = self.E[e].memset(self.junk[e][:, 0:1], 0.0)
            self.sig(e, ins)
        toks = [(c, c.v) for c in self.cnt.values()] + [(s, s.v) for s in self.dma_sems]
        for e in ("pe", "act", "dve", "pool", "sp"):
            for t in toks:
                self.wait(e, t)


def build_program(debug=False, stop_after="C"):
    nc = bass.Bass("TRN2", target_bir_lowering=False)
    es = ExitStack()
    k = KB(nc, es)
    pe, act, dve, pool, sp = nc.tensor, nc.scalar, nc.vector, nc.gpsimd, nc.sync

    def din(name, shape, dt=F32):
        return nc.dram_tensor(name, list(shape), dt, kind="ExternalInput").ap()

    def dscr(name, shape, dt, dbg=False):
        kind = "ExternalOutput" if (debug and dbg) else "Internal"
        return nc.dram_tensor(name, list(shape), dt, kind=kind).ap()

    x_r0 = din("x_r0", [NCH * P, DC * TW])
    x_own = din("x_own", [NCH * P, DC * TW])
    out_d = nc.dram_tensor("out", [NCH * P, DC * TW], F32, kind="ExternalOutput").ap()
    gains_d = din("gains", [P, 7 * DC])
    convw_d = din("convw", [P, 3 * DC])
    vis_d = din("vis", [P, NKT])
    halo_d = din("halo_on", [P, 1])
    lamv_d = din("lamv", [1, 4 * 64])
    subln_d = din("subln", [1, P])
    ident_d = din("ident", [P, P])
    masks_d = din("masks", [P, 4 * TW])

    wshapes = {}
    for l in range(2):
        for f in (1, 2):
            wshapes[f"wgu{l}{f}"] = FC * P * 2 * DC * P
            wshapes[f"wd{l}{f}"] = DC * P * FC * P
    wshapes["wqk"] = 16 * P * DC * P
    wshapes["wv"] = P * DC * D
    wshapes["wo"] = P * DC * D
    wshapes["wcin"] = DC * P * 3 * DC * P
    wshapes["wcout"] = P * DC * D
    w32 = {n: din(n, [sz // 2048, 2048]) for n, sz in wshapes.items()}
    w16 = {n: dscr(n + "_b", [sz // 2048, 2048], BF16) for n, sz in wshapes.items()}

    def wview(name, line):
        t = w16[name]
        if line == 2048:
            return t
        return t.rearrange("a b -> (a b)").rearrange("(r c) -> r c", c=line)

    k_scr = dscr("k_scr", [NH * P, 2 * HALF], BF16, dbg=True)
    v_scr = dscr("v_scr", [NH * P, NKT * VW], BF16, dbg=True)
    q_scr = dscr("q_scr", [NH * P, QCOLS], BF16, dbg=True)
    ot_scr = dscr("ot_scr", [NH * P, QCOLS], BF16, dbg=True)
    xs = dscr("xs", [NCH * P, DC * TW], F32, dbg=True)
    xs_h = dscr("xs_h", [P, DC * P], F32, dbg=True)

    gains = k.sb("gains_s", [P, 7, DC], F32)
    convw = k.sb("convw_s", [P, 3, DC], F32)
    vis = k.sb("vis_s", [P, NKT], F32)
    halo_on = k.sb("halo_s", [P, 1], F32)
    lamv = k.sb("lamv_s", [P, 4, 64], F32)
    gsub = k.sb("gsub_s", [P, P], F32)
    ident = k.sb("ident_s", [P, P], BF16)
    masks = k.sb("masks_s", [P, 4, TW], BF16)
    ones_bf = k.sb("ones_bf", [P, P], BF16)
    ones81 = k.sb("ones81", [P, NH, 1], F32)
    nlam = k.sb("nlam", [P, 1], F32)
    mhalf = k.sb("mhalf", [P, 4], F32)
    lam_t = k.sb("lam_t", [P, 4], F32)
    lam_p = k.sb("lam_p", [P, 2, 64], F32)
    k.junk = {e: k.sb("junk_" + e, [P, 4], F32) for e in ("act", "dve", "pool")}

    accp = [k.ps(f"accp{i}", [P, 2, TW]) for i in range(2)]
    dnb = [k.ps(f"dnb{i}", [P, TW]) for i in range(2)]
    ssb = k.ps("ssb", [P, TW])
    tpb = k.ps("tpb", [P, 2 * TW], BF16)
    acc = Ring([accp[0][:, 0, :], accp[0][:, 1, :], accp[1][:, 0, :], accp[1][:, 1, :]])
    dnr = Ring([dnb[0][:], dnb[1][:]])

    s_const = k.newdsem("s_const")
    s_constp = k.newdsem("s_constp")
    s_castq = [k.newdsem(f"s_castq{i}") for i in range(6)]
    s_castC = k.newdsem("s_castC")
    s_x = [k.newdsem(f"s_x{i}") for i in range(2)]
    s_xst = [k.newdsem(f"s_xst{i}") for i in range(2)]
    s_wgu = [k.newdsem(f"s_wgu{i}") for i in range(3)]
    s_wd = [k.newdsem(f"s_wd{i}") for i in range(2)]
    s_res = k.newdsem("s_res")
    s_res2 = k.newdsem("s_res2")
    s_st = [k.newdsem(f"s_st{i}") for i in range(2)]
    s_vst = k.newdsem("s_vst")
    s_hd = [k.newdsem(f"s_hd{i}") for i in range(2)]
    s_ot = [k.newdsem(f"s_ot{i}") for i in range(2)]
    s_otl = k.newdsem("s_otl")
    s_out = [k.newdsem(f"s_out{i}") for i in range(2)]

    class WRing:
        def __init__(self, tiles, sems):
            self.tiles, self.sems, self.nb = tiles, sems, len(tiles)
            self.n_load = 0
            self.n_cons = 0
            self.loaded = {}
            self.rel = {}

        def load(self, src, dst_fn):
            n = self.n_load
            s = n % self.nb
            if n >= self.nb:
                k.wait("sp", self.rel[n - self.nb])
            self.loaded[n] = k.dma("sp", dst_fn(self.tiles[s]), src, self.sems[s])
            self.n_load += 1

        def take(self):
            n = self.n_cons
            self.n_cons += 1
            k.wait("pe", self.loaded[n])
            return n, self.tiles[n % self.nb]

    class NS:
        pass

    B = NS()

    def alloc_ffn(stack, tag):
        def sbt(name, shape, dt):
            return stack.enter_context(nc.sbuf_tensor(name + tag, list(shape), dt))
        B.xc = [sbt(f"xc{i}", [P, DC, TW], F32) for i in range(2)]
        B.hT = sbt("hT", [P, DC, TW], BF16)
        B.rstd = sbt("rstd", [P, TW], F32)
        B.sq = [sbt(f"sq{i}", [P, TW], BF16) for i in range(4)]
        B.aT = sbt("aT", [P, FC, TW], BF16)
        B.sg = [sbt(f"sg{i}", [P, TW], F32) for i in range(2)]
        wgu_t = [sbt(f"wgu_t{i}", [P, 3, DC, P], BF16) for i in range(3)]
        wd_t = [sbt(f"wd_t{i}", [P, FC, P], BF16) for i in range(2)]
        B.wgu_r = WRing(wgu_t, s_wgu)
        B.wd_r = WRing(wd_t, s_wd)

    st = {"hT_free": None, "aT_free": None, "rstd_free": None, "ssb_free": None, "sg_free": [None, None],
          "sq_free": [None] * 4, "sq_n": 0}

    k.dma("sp", gains[:].rearrange("p a b -> p (a b)"), gains_d, s_const)
    k.dma("sp", convw[:].rearrange("p a b -> p (a b)"), convw_d, s_const)
    k.dma("sp", vis[:], vis_d, s_const)
    k.dma("sp", halo_on[:], halo_d, s_const)
    k.dma("sp", lamv[:].rearrange("p a b -> p (a b)"), lamv_d.partition_broadcast(P), s_const)
    k.dma("sp", gsub[:], subln_d.partition_broadcast(P), s_const)
    tok_const = (s_const, s_const.v)
    k.dma("pool", ident[:], ident_d, s_constp)
    k.dma("pool", masks[:].rearrange("p a b -> p (a b)"), masks_d, s_constp)
    tok_constp = (s_constp, s_constp.v)
    orderA = ["wgu01", "wd01", "wqk", "wv"]
    orderC = ["wo", "wgu02", "wd02", "wgu11", "wd11", "wcin", "wcout", "wgu12", "wd12"]
    cast_tok = {}
    nrow = wshapes["wgu01"] // 2048
    qr = nrow // 4
    prev_cast = [tok_constp]

    def cast(dst, src, sem):
        k.wait("pool", prev_cast[0])
        t = k.dma("pool", dst, src, sem)
        prev_cast[0] = t
        return t
    for i in range(4):
        cast_tok[("wgu01", i)] = cast(w16["wgu01"][i * qr:(i + 1) * qr, :], w32["wgu01"][i * qr:(i + 1) * qr, :],
                                      s_castq[i])
    cast_tok["wd01"] = cast(w16["wd01"], w32["wd01"], s_castq[4])
    cast(w16["wqk"], w32["wqk"], s_castq[5])
    cast_tok["wqkv"] = cast(w16["wv"], w32["wv"], s_castq[5])
    castA_all = [cast_tok[("wgu01", i)] for i in range(4)] + [cast_tok["wd01"], cast_tok["wqkv"]]
    k.wait("pool", tok_constp)
    for t_ in castA_all:
        k.wait("pool", t_)
    castC_pieces = []
    for n in orderC:
        nr = wshapes[n] // 2048
        if nr > 1024:
            castC_pieces.append((w16[n][0:nr // 2, :], w32[n][0:nr // 2, :]))
            castC_pieces.append((w16[n][nr // 2:nr, :], w32[n][nr // 2:nr, :]))
        else:
            castC_pieces.append((w16[n], w32[n]))

    dve.memset(ones_bf[:], 1.0)
    dve.memset(mhalf[:], -0.5)
    k.sig("dve", dve.memset(ones81[:], 1.0))
    tok_pool_const = (k.cnt["dve"], k.cnt["dve"].v)
    k.wait("dve", tok_const)
    dve.tensor_tensor(out=lam_p[:, 0, :], in0=lamv[:, 0, :], in1=lamv[:, 1, :], op=ALU.mult)
    k.wait("dve", k.sig("dve", dve.tensor_tensor(out=lam_p[:, 1, :], in0=lamv[:, 2, :], in1=lamv[:, 3, :],
                                                 op=ALU.mult)))
    t_l = k.sig("dve", dve.reduce_sum(out=lam_t[:, 0:2], in_=lam_p[:], axis=AX.X))
    k.wait("act", t_l)
    t_l2 = k.sig("act", act.activation(out=lam_t[:, 2:4], in_=lam_t[:, 0:2], func=AF.Exp))
    k.wait("dve", t_l2)
    k.wait("dve", k.sig("dve", dve.tensor_tensor(out=nlam[:], in0=lam_t[:, 3:4], in1=lam_t[:, 2:3],
                                                 op=ALU.subtract)))
    k.wait("dve", k.sig("dve", dve.tensor_scalar(out=nlam[:], in0=nlam[:], scalar1=-LAMBDA_INIT, scalar2=None,
                                                 op0=ALU.add)))
    dve.tensor_scalar(out=gsub[:], in0=gsub[:], scalar1=1.0 - LAMBDA_INIT, scalar2=None, op0=ALU.mult)
    for e in ("pe", "act", "dve", "pool"):
        k.wait(e, tok_const)
        k.wait(e, tok_constp)
        k.wait(e, tok_pool_const)

    def stats_step(xv, W, dc, x_tok):
        s = st["sq_n"] % len(B.sq)
        st["sq_n"] += 1
        k.wait("act", x_tok)
        k.wait("act", st["sq_free"][s])
        t_sq = k.sig("act", act.activation(out=B.sq[s][:, :W], in_=xv[:, dc, :W], func=AF.Square))

        def run():
            k.wait("pe", t_sq)
            if dc == 0:
                k.wait("pe", st["ssb_free"])
            t = k.sig("pe", pe.matmul(ssb[:, :W], lhsT=ones_bf[:], rhs=B.sq[s][:, :W], start=(dc == 0),
                                      stop=(dc == DC - 1)))
            st["sq_free"][s] = t
            return t
        return run

    def norm(xv, W, gidx, x_tok, write_h=True, ss_tok=None):
        last_pe = ss_tok
        k.wait("act", x_tok)
        for dc in (range(DC) if ss_tok is None else []):
            s = st["sq_n"] % len(B.sq)
            st["sq_n"] += 1
            k.wait("act", st["sq_free"][s])
            t_sq = k.sig("act", act.activation(out=B.sq[s][:, :W], in_=xv[:, dc, :W], func=AF.Square))
            k.wait("pe", t_sq)
            if dc == 0:
                k.wait("pe", st["ssb_free"])
            last_pe = k.sig("pe", pe.matmul(ssb[:, :W], lhsT=ones_bf[:], rhs=B.sq[s][:, :W],
                                            start=(dc == 0), stop=(dc == DC - 1)))
            st["sq_free"][s] = last_pe
        k.wait("act", last_pe)
        k.wait("act", st["rstd_free"])
        t_a = k.sig("act", act.activation(out=B.rstd[:, :W], in_=ssb[:, :W], func=AF.Sqrt, scale=1.0 / D,
                                          bias=eps_t[:, 0:1]))
        st["ssb_free"] = t_a
        k.wait("dve", t_a)
        k.wait("dve", x_tok)
        t_d = k.sig("dve", dve.reciprocal(out=B.rstd[:, :W], in_=B.rstd[:, :W]))
        if not write_h:
            return [t_d] * DC
        k.wait("dve", st["hT_free"])
        toks = []
        for dc in range(DC):
            toks.append(k.sig("dve", dve.scalar_tensor_tensor(
                out=B.hT[:, dc, :W], in0=xv[:, dc, :W], scalar=gains[:, gidx, dc:dc + 1], in1=B.rstd[:, :W],
                op0=ALU.mult, op1=ALU.mult)))
        st["rstd_free"] = toks[-1]
        return toks

    def acq():
        A = acc.tiles[acc.n % acc.nb]
        k.wait("pe", acc.prev_free())
        nA = acc.n
        acc.n += 1
        return A, nA

    def ffn(xv, W, x_tok, wname_gu, wname_d, gidx, ss_tok=None, first=False):
        wg_src = wview(wname_gu, 2048)
        wd_src = wview(wname_d, FC * P)
        wg_dst = lambda t: t[:, 0:2, :, :].rearrange("p a b c -> p (a b c)")
        wd_dst = lambda t: t[:].rearrange("p a b -> p (a b)")
        npre_g = min(B.wgu_r.nb, FC)

        def cast_gate(fc):
            if first:
                k.wait("sp", cast_tok[("wgu01", min(3, (fc * P) // qr))])
                k.wait("sp", cast_tok[("wgu01", min(3, ((fc + 1) * P - 1) // qr))])
        for fc in range(npre_g):
            cast_gate(fc)
            B.wgu_r.load(wg_src[fc * P:(fc + 1) * P, :], wg_dst)
        h_toks = norm(xv, W, gidx, x_tok, ss_tok=ss_tok)
        npre_d = min(B.wd_r.nb, DC)
        state = {"last_d": None, "last_pe": None, "wd_pref": False}

        def evac(fc, G, nG, U, nU, t_pe):
            s = fc % 2
            k.wait("act", t_pe)
            k.wait("act", st["sg_free"][s])
            t_a = k.sig("act", act.activation(out=B.sg[s][:, :W], in_=G[:, :W], func=AF.Silu))
            k.wait("dve", t_a)
            if fc == 0:
                k.wait("dve", st["aT_free"])
            t_d = k.sig("dve", dve.tensor_tensor(out=B.aT[:, fc, :W], in0=B.sg[s][:, :W], in1=U[:, :W],
                                                 op=ALU.mult))
            acc.free[nG] = t_d
            acc.free[nU] = t_d
            st["sg_free"][s] = t_d
            state["last_d"] = t_d

        def after_fc(fc, n, t_pe):
            state["last_pe"] = t_pe
            B.wgu_r.rel[n] = t_pe
            if fc + npre_g < FC:
                f2 = fc + npre_g
                cast_gate(f2)
                B.wgu_r.load(wg_src[f2 * P:(f2 + 1) * P, :], wg_dst)
            if fc == (FC - 1 if first else 8):
                if first:
                    k.wait("sp", cast_tok["wd01"])
                for do in range(npre_d):
                    B.wd_r.load(wd_src[do * P:(do + 1) * P, :], wd_dst)

        n0, w0 = B.wgu_r.take()
        n1, w1 = B.wgu_r.take()
        slots = [acq() for _ in range(4)]
        wts = [w0, w0, w1, w1]
        toks = [None] * 4
        for dc in range(DC):
            k.wait("pe", h_toks[dc])
            for i in range(4):
                ins = pe.matmul(slots[i][0][:, :W], lhsT=wts[i][:, i % 2, dc, :], rhs=B.hT[:, dc, :W],
                                start=(dc == 0), stop=(dc == DC - 1))
                if dc == DC - 1 and i % 2 == 1:
                    toks[i] = k.sig("pe", ins)
        after_fc(0, n0, toks[1])
        after_fc(1, n1, toks[3])
        evac(0, slots[0][0], slots[0][1], slots[1][0], slots[1][1], toks[1])
        evac(1, slots[2][0], slots[2][1], slots[3][0], slots[3][1], toks[3])
        for fc in range(2, FC):
            n, wt = B.wgu_r.take()
            G, nG = acq()
            U, nU = acq()
            for dc in range(DC):
                pe.matmul(G[:, :W], lhsT=wt[:, 0, dc, :], rhs=B.hT[:, dc, :W], start=(dc == 0), stop=(dc == DC - 1))
            for dc in range(DC):
                ins = pe.matmul(U[:, :W], lhsT=wt[:, 1, dc, :], rhs=B.hT[:, dc, :W], start=(dc == 0),
                                stop=(dc == DC - 1))
            t_pe = k.sig("pe", ins)
            after_fc(fc, n, t_pe)
            evac(fc, G, nG, U, nU, t_pe)
        st["hT_free"] = state["last_pe"]
        last_d = state["last_d"]
        x_done = None
        pending = None
        for do in range(DC):
            n, wt = B.wd_r.take()
            k.wait("pe", last_d)
            Y = dnr.tiles[dnr.n % dnr.nb]
            k.wait("pe", dnr.prev_free()); nY = dnr.n; dnr.n += 1
            for fc in range(FC):
                ins = pe.matmul(Y[:, :W], lhsT=wt[:, fc, :], rhs=B.aT[:, fc, :W], start=(fc == 0), stop=(fc == FC - 1))
            t_pe = k.sig("pe", ins)
            B.wd_r.rel[n] = t_pe
            if pending is not None:
                pending()
            if do + npre_d < DC:
                d2 = do + npre_d
                B.wd_r.load(wd_src[d2 * P:(d2 + 1) * P, :], wd_dst)
            k.wait("dve", t_pe)
            x_done = k.sig("dve", dve.scalar_tensor_tensor(out=xv[:, do, :W], in0=Y[:, :W], scalar=0.5,
                                                           in1=xv[:, do, :W], op0=ALU.mult, op1=ALU.add))
            dnr.free[nY] = x_done
            st["aT_free"] = t_pe
            pending = stats_step(xv, W, do, x_done)
        ss_next = pending()
        return x_done, ss_next

    eps_t = k.sb("eps_t", [P, 1], F32)
    k.sig("dve", dve.memset(eps_t[:], EPS))
    k.wait("act", (k.cnt["dve"], k.cnt["dve"].v))

    phaseA = ExitStack()
    alloc_ffn(phaseA, "A")
    wqk_s = phaseA.enter_context(nc.sbuf_tensor("wqk_s", [P, 16, DC * P], BF16))
    wv_s = phaseA.enter_context(nc.sbuf_tensor("wv_s", [P, DC, D], BF16))
    kst = [phaseA.enter_context(nc.sbuf_tensor(f"kst{i}", [P, TW], BF16)) for i in range(2)]
    vst = phaseA.enter_context(nc.sbuf_tensor("vst", [P, NH, 4, VW], BF16))

    resA = {}

    def load_resA():
        k.wait("sp", cast_tok["wqkv"])
        k.dma("sp", wqk_s[:], wview("wqk", DC * P).rearrange("(kc p) x -> p kc x", p=P), s_res)
        k.dma("sp", wv_s[:].rearrange("p a b -> p (a b)"), wview("wv", DC * D), s_res)
        resA["tok"] = (s_res, s_res.v)

    def x_src(g):
        src = x_r0 if g < NCH else x_own
        c = g % NCH
        return src[c * P:(c + 1) * P, :]

    x_ld = {}
    x_slot_free = [[], []]

    def issue_xload(g):
        s = g % 2
        for t in x_slot_free[s]:
            k.wait("sp", t)
        x_slot_free[s] = []
        x_ld[g] = k.dma("sp", B.xc[s][:].rearrange("p a b -> p (a b)"), x_src(g), s_x[s])

    NG = 2 * NCH
    issue_xload(0)
    issue_xload(1)
    st_i = 0
    st_free = [None, None]
    vst_free = None
    for g in range(NG):
        s = g % 2
        xv = B.xc[s]
        own = g >= NCH
        c = g % NCH
        tok0 = g * TW
        x_done, ss_n = ffn(xv, TW, x_ld[g], "wgu01", "wd01", 0, first=(g == 0))
        if g == 0:
            load_resA()
        tok_resA = resA["tok"]
        if own:
            k.wait("act", x_done)
            t_xs = k.dma("act", xs[c * P:(c + 1) * P, :], xv[:].rearrange("p a b -> p (a b)"), s_xst[s])
            x_slot_free[s].append(t_xs)
        if g == NCH - 1:
            k.wait("act", x_done)
            t_xs = k.dma("act", xs_h.rearrange("p (a b) -> p a b", a=DC), xv[:, :, TW - P:TW], s_xst[s])
            x_slot_free[s].append(t_xs)
        h_toks = norm(xv, TW, 1, x_done, ss_tok=ss_n)
        t_h = h_toks[-1]
        x_slot_free[s].append(t_h)
        k.wait("pe", tok_resA)
        first_group = True
        jobs = [("k", kc) for kc in range(NH)]
        if own:
            jobs += [("q", kc) for kc in range(NH)]
        elif g == NCH - 1:
            jobs += [("qh", kc) for kc in range(NH)]
        last_pe = None
        for kind, kc in jobs:
            A = acc.tiles[acc.n % acc.nb]
            k.wait("pe", acc.prev_free()); nA = acc.n; acc.n += 1
            widx = (8 + kc) if kind == "k" else kc
            if kind == "qh":
                cs, W = TW - P, P
            else:
                cs, W = 0, TW
            for dc in range(DC):
                if first_group:
                    k.wait("pe", h_toks[dc])
                ins = pe.matmul(A[:, :W], lhsT=wqk_s[:, widx, dc * P:(dc + 1) * P], rhs=B.hT[:, dc, cs:cs + W],
                                start=(dc == 0), stop=(dc == DC - 1))
            first_group = False
            t_pe = k.sig("pe", ins)
            last_pe = t_pe
            ss_ = st_i % 2
            st_i += 1
            k.wait("act", t_pe)
            k.wait("act", st_free[ss_])
            t_a = k.sig("act", act.copy(out=kst[ss_][:, :W], in_=A[:, :W]))
            acc.free[nA] = t_a
            k.wait("act", t_a)
            if kind == "k":
                dst = k_scr[kc * P:(kc + 1) * P, tok0:tok0 + TW]
            elif kind == "q":
                dst = q_scr[kc * P:(kc + 1) * P, P + c * TW:P + (c + 1) * TW]
            else:
                dst = q_scr[kc * P:(kc + 1) * P, 0:P]
            st_free[ss_] = k.dma("act", dst, kst[ss_][:, :W], s_st[ss_])
        for tt in range(4):
            j = g * 4 + tt
            for half in range(2):
                A = acc.tiles[acc.n % acc.nb]
                k.wait("pe", acc.prev_free()); nA = acc.n; acc.n += 1
                for dc in range(DC):
                    ins = pe.matmul(A[:, :TW], lhsT=B.hT[:, dc, tt * P:(tt + 1) * P],
                                    rhs=wv_s[:, dc, half * TW:(half + 1) * TW], start=(dc == 0), stop=(dc == DC - 1))
                t_pe = k.sig("pe", ins)
                last_pe = t_pe
                k.wait("act", t_pe)
                if tt == 0 and half == 0:
                    k.wait("act", vst_free)
                    k.wait("dve", vst_free)
                t_a = k.sig("act", act.mul(out=vst[:, half * 4:(half + 1) * 4, tt, 0:P],
                                           in_=A[:, :TW].rearrange("p (h e) -> p h e", h=4), mul=vis[:, j:j + 1]))
                acc.free[nA] = t_a
            t_v1 = k.sig("dve", dve.tensor_scalar(out=vst[:, :, tt, P:VW], in0=ones81[:], scalar1=vis[:, j:j + 1],
                                                  scalar2=None, op0=ALU.mult))
        st["hT_free"] = last_pe
        k.wait("act", t_a)
        k.wait("act", t_v1)
        vst_free = k.dma("act", v_scr.rearrange("(h p) c -> p h c", p=P)[:, :, g * 4 * VW:(g + 1) * 4 * VW],
                         vst[:].rearrange("p h t e -> p h (t e)"), s_vst)
        if g + 2 < NG:
            issue_xload(g + 2)
    k.barrier()
    phaseA.close()

    if stop_after == "A":
        es_close(k, es)
        return nc

    phaseB = ExitStack()
    kt = [phaseB.enter_context(nc.sbuf_tensor(f"kt{i}", [P, 2 * HALF], BF16)) for i in range(2)]
    vt = [phaseB.enter_context(nc.sbuf_tensor(f"vt{i}", [P, NKT, VW], BF16)) for i in range(2)]
    qt = [phaseB.enter_context(nc.sbuf_tensor(f"qt{i}", [P, QCOLS], BF16)) for i in range(2)]
    oth = [phaseB.enter_context(nc.sbuf_tensor(f"oth{i}", [P, QCOLS], BF16)) for i in range(2)]
    Et = [phaseB.enter_context(nc.sbuf_tensor(f"Et{i}", [P, 2, TW], BF16)) for i in range(4)]
    on_st = phaseB.enter_context(nc.sbuf_tensor("on_st", [P, 4, P], BF16))
    o_tmp = phaseB.enter_context(nc.sbuf_tensor("o_tmp", [P, 4, P], F32))
    a_tmp = phaseB.enter_context(nc.sbuf_tensor("a_tmp", [P, 4, P], F32))
    j_tmp = phaseB.enter_context(nc.sbuf_tensor("j_tmp", [P, 4, P], F32))
    rs8 = phaseB.enter_context(nc.sbuf_tensor("rs8", [P, 8], F32))
    o_sb = phaseB.enter_context(nc.sbuf_tensor("o_sb", [P, 3, 3 * VW], F32))
    rr8 = phaseB.enter_context(nc.sbuf_tensor("rr8", [P, 8], F32))
    nl4 = phaseB.enter_context(nc.sbuf_tensor("nl4", [P, 4], F32))
    ssq = phaseB.enter_context(nc.sbuf_tensor("ssq", [P, 4], F32))
    sd = phaseB.enter_context(nc.sbuf_tensor("sd", [P, 4], F32))

    obanks = [dnb[0], dnb[1], ssb]

    def oacc(c, t):
        i = c * 4 + t
        return obanks[i // 3][:, (i % 3) * VW:(i % 3 + 1) * VW], i // 3

    sring = Ring([accp[0], accp[1]])
    ering = Ring(Et)
    hd_free = [None, None]
    oth_free = [None, None]
    of_free = None
    onst_free = None
    otmp_free = None
    tp_free = None
    ssq_free = None
    sd_free = None

    qlist = [("h", 0)] + [("o", i) for i in range(NCH)]
    att_step = [0]
    deferred = []
    t_c = None
    for h in range(NH):
        hs = h % 2
        k.wait("sp", hd_free[hs])
        k.dma("sp", kt[hs][:], k_scr[h * P:(h + 1) * P, :], s_hd[hs])
        k.dma("sp", vt[hs][:].rearrange("p a b -> p (a b)"), v_scr[h * P:(h + 1) * P, :], s_hd[hs])
        t_ld = k.dma("sp", qt[hs][:], q_scr[h * P:(h + 1) * P, :], s_hd[hs])
        KT, VT, QT, OTH = kt[hs], vt[hs], qt[hs], oth[hs]
        k.wait("dve", oth_free[hs])
        last_head_pe = None
        pendq = []

        def emit_av(pd):
            nonlocal last_head_pe
            Etile, nE, tok_e, j, first, last, tmin, ctx = pd
            nt_ = ctx["nt"]
            bs_ = ctx["bs"]
            k.wait("pe", tok_e)
            if first:
                k.wait("pe", of_free)
            ins = None
            for c in range(2):
                for t in range(tmin, nt_):
                    o_ap, b = oacc(c, t)
                    stflag = False
                    if first and not bs_[b]:
                        stflag = True
                        bs_[b] = True
                    ins = pe.matmul(o_ap, lhsT=Etile[:, c, t * P:(t + 1) * P], rhs=VT[:, j, :], start=stflag,
                                    stop=last, skip_group_check=True)
            t_av = k.sig("pe", ins)
            ering.free[nE] = t_av
            last_head_pe = t_av
            if last:
                ctx["fin"](t_av)
            return t_av

        for kind, qi in qlist:
            if kind == "h":
                W, qoff = P, 0
                tiles = [(j, None) for j in range(31)] + [(31, 0)]
            else:
                W, qoff = TW, P + qi * TW
                tiles = [(j, None) for j in range(32 + 4 * qi)] + [(32 + 4 * qi + m, m) for m in range(4)]
            nt = W // P
            ctx = {"nt": nt, "bs": [False, False, False]}

            def finalize(t_av_last, nt=nt, W=W, qoff=qoff, OTH=OTH):
                nonlocal of_free, otmp_free, ssq_free, sd_free
                k.wait("dve", t_av_last)
                k.wait("dve", otmp_free)
                k.wait("dve", ssq_free)
                if nt == 4:
                    dve.tensor_copy(out=o_sb[:, 0, :], in_=obanks[0][:, 0:3 * VW])
                    dve.tensor_copy(out=o_sb[:, 1, :], in_=obanks[1][:, 0:3 * VW])
                    ins = dve.tensor_copy(out=o_sb[:, 2, 0:2 * VW], in_=obanks[2][:, 0:2 * VW])
                else:
                    dve.tensor_copy(out=o_sb[:, 0, 0:VW], in_=obanks[0][:, 0:VW])
                    ins = dve.tensor_copy(out=o_sb[:, 1, VW:2 * VW], in_=obanks[1][:, VW:2 * VW])
                of_free = k.sig("dve", ins)
                k.wait("dve", of_free)

                def oacc_s(c, t):
                    i = c * 4 + t
                    return o_sb[:, i // 3, (i % 3) * VW:(i % 3 + 1) * VW]
                for t in range(nt):
                    o1 = oacc_s(0, t)
                    o2 = oacc_s(1, t)
                    dve.tensor_scalar(out=rs8[:, t:t + 1], in0=o1[:, P:VW], scalar1=1e-30, scalar2=None, op0=ALU.max)
                    ins = dve.tensor_scalar(out=rs8[:, 4 + t:5 + t], in0=o2[:, P:VW], scalar1=1e-30, scalar2=None,
                                            op0=ALU.max)
                k.wait("dve", k.sig("dve", ins))
                if nt == 4:
                    ins = dve.reciprocal(out=rr8[:, 0:8], in_=rs8[:, 0:8])
                else:
                    dve.reciprocal(out=rr8[:, 0:1], in_=rs8[:, 0:1])
                    ins = dve.reciprocal(out=rr8[:, 4:5], in_=rs8[:, 4:5])
                k.wait("dve", k.sig("dve", ins))
                ins = dve.tensor_scalar(out=nl4[:, 0:nt], in0=rr8[:, 4:4 + nt], scalar1=nlam[:, 0:1], scalar2=None,
                                        op0=ALU.mult)
                k.wait("dve", k.sig("dve", ins))
                for t in range(nt):
                    o1 = oacc_s(0, t)
                    dve.tensor_scalar(out=a_tmp[:, t, :], in0=o1[:, 0:P], scalar1=rr8[:, t:t + 1], scalar2=None,
                                      op0=ALU.mult)
                for t in range(nt):
                    o2 = oacc_s(1, t)
                    ins = dve.scalar_tensor_tensor(out=o_tmp[:, t, :], in0=o2[:, 0:P], scalar=nl4[:, t:t + 1],
                                                   in1=a_tmp[:, t, :], op0=ALU.mult, op1=ALU.add)
                t_f1 = k.sig("dve", ins)
                k.wait("dve", t_f1)
                dve.tensor_tensor(out=j_tmp[:, :nt, :], in0=o_tmp[:, :nt, :], in1=o_tmp[:, :nt, :], op=ALU.mult)
                k.wait("dve", k.sig("dve", dve.reduce_sum(out=ssq[:, :nt], in_=j_tmp[:, :nt, :], axis=AX.X)))
                t_ss = k.sig("dve", dve.tensor_scalar(out=ssq[:, :nt], in0=ssq[:, :nt], scalar1=1.0 / P, scalar2=EPS,
                                                      op0=ALU.mult, op1=ALU.add))
                k.wait("pool", t_ss)
                k.wait("pool", sd_free)
                t_sd = k.sig("pool", pool.tensor_tensor(out=sd[:, :nt], in0=ssq[:, :nt], in1=mhalf[:, :nt], op=ALU.pow))
                att_step[0] += 1
                if castC_pieces and att_step[0] % 4 == 1:
                    dst_, src_ = castC_pieces.pop(0)
                    cast(dst_, src_, s_castC)
                ssq_free = t_sd
                k.wait("dve", t_sd)
                k.wait("dve", onst_free)
                for t in range(nt):
                    ins = dve.scalar_tensor_tensor(out=on_st[:, t, :], in0=o_tmp[:, t, :], scalar=sd[:, t:t + 1],
                                                   in1=gsub[:], op0=ALU.mult, op1=ALU.mult)
                t_f2 = k.sig("dve", ins)
                otmp_free = t_f2
                sd_free = t_f2

                def make_deferred(t_f2=t_f2, nt=nt, OTH=OTH, qoff=qoff, W=W):
                    def run():
                        nonlocal last_head_pe, onst_free, tp_free, t_c
                        k.wait("pe", t_f2)
                        k.wait("pe", tp_free)
                        for t in range(nt):
                            ins = pe.transpose(out=tpb[:, t * P:(t + 1) * P], in_=on_st[:, t, :], identity=ident[:])
                        t_t = k.sig("pe", ins)
                        last_head_pe = t_t
                        onst_free = t_t
                        k.wait("dve", t_t)
                        t_c = k.sig("dve", dve.tensor_copy(out=OTH[:, qoff:qoff + W], in_=tpb[:, :W]))
                        tp_free = t_c
                    return run
                deferred.append(make_deferred())
            ctx["fin"] = finalize
            for idx, (j, m) in enumerate(tiles):
                Sb = sring.tiles[sring.n % sring.nb]
                k.wait("pe", sring.prev_free()); nS = sring.n; sring.n += 1
                k.wait("pe", t_ld)
                c0 = P * m if m is not None else 0
                pe.matmul(Sb[:, 0, c0:W], lhsT=KT[0:64, j * P:(j + 1) * P], rhs=QT[0:64, qoff + c0:qoff + W],
                          start=True, stop=True)
                t_s = k.sig("pe", pe.matmul(Sb[:, 1, c0:W], lhsT=KT[64:128, j * P:(j + 1) * P],
                                            rhs=QT[64:128, qoff + c0:qoff + W], start=True, stop=True))
                Etile = ering.tiles[ering.n % ering.nb]
                k.wait("act", t_s)
                k.wait("act", ering.prev_free()); nE = ering.n; ering.n += 1
                t_e = k.sig("act", act.activation(out=Etile[:, :, c0:W], in_=Sb[:, :, c0:W], func=AF.Exp,
                                                  scale=0.125))
                sring.free[nS] = t_e
                if m is not None:
                    k.wait("dve", t_e)
                    dve.tensor_tensor(out=Etile[:, 0, c0:W], in0=Etile[:, 0, c0:W], in1=masks[:, m, c0:W],
                                      op=ALU.mult)
                    t_e = k.sig("dve", dve.tensor_tensor(out=Etile[:, 1, c0:W], in0=Etile[:, 1, c0:W],
                                                          in1=masks[:, m, c0:W], op=ALU.mult))
                pendq.append((Etile, nE, t_e, j, idx == 0, idx == len(tiles) - 1, (m or 0), ctx))
                if len(pendq) > 2:
                    emit_av(pendq.pop(0))
                if idx == 17:
                    while deferred:
                        deferred.pop(0)()
        while pendq:
            emit_av(pendq.pop(0))
        hd_free[hs] = last_head_pe

        def make_store(h=h, hs=hs, OTH=OTH):
            def run():
                k.wait("pool", t_c)
                oth_free[hs] = k.dma("pool", ot_scr[h * P:(h + 1) * P, :], OTH[:], s_ot[hs])
            return run
        deferred.append(make_store())
    while deferred:
        deferred.pop(0)()
    while castC_pieces:
        dst_, src_ = castC_pieces.pop(0)
        cast(dst_, src_, s_castC)
    tok_castC = (s_castC, s_castC.v)
    k.barrier()
    phaseB.close()
    if stop_after == "B":
        es_close(k, es)
        return nc

    phaseC = ExitStack()
    alloc_ffn(phaseC, "C")
    wo_s = phaseC.enter_context(nc.sbuf_tensor("wo_s", [P, DC, D], BF16))
    wco_s = phaseC.enter_context(nc.sbuf_tensor("wco_s", [P, DC, D], BF16))
    otc = [phaseC.enter_context(nc.sbuf_tensor(f"otc{i}", [P, NH, TW], BF16)) for i in range(2)]
    ubuf = phaseC.enter_context(nc.sbuf_tensor("ubuf", [P, DC, TW + 2], F32))
    zT = phaseC.enter_context(nc.sbuf_tensor("zT", [P, DC, TW], BF16))
    gcs = [phaseC.enter_context(nc.sbuf_tensor(f"gcs{i}", [P, TW], F32)) for i in range(2)]
    ytmp = phaseC.enter_context(nc.sbuf_tensor("ytmp", [P, TW], F32))

    k.wait("sp", tok_castC)
    k.dma("sp", wo_s[:].rearrange("p a b -> p (a b)"), wview("wo", DC * D), s_res)
    tok_resC = (s_res, s_res.v)
    k.sig("dve", dve.memset(ubuf[:, :, 0:2], 0.0))

    clist = [("h", 0)] + [("o", i) for i in range(NCH)]
    xslot_free = [[], []]
    cl_ld = {}

    def issue_cload(ci):
        kind, i = clist[ci]
        s = ci % 2
        for t in xslot_free[s]:
            k.wait("sp", t)
        xslot_free[s] = []
        if kind == "h":
            k.dma("sp", B.xc[s][:, :, 0:P], xs_h.rearrange("p (a b) -> p a b", a=DC), s_x[s])
            t2 = k.dma("sp", otc[s][:, :, 0:P], ot_scr.rearrange("(h p) c -> p h c", p=P)[:, :, 0:P], s_x[s])
        else:
            k.dma("sp", B.xc[s][:].rearrange("p a b -> p (a b)"), xs[i * P:(i + 1) * P, :], s_x[s])
            t2 = k.dma("sp", otc[s][:], ot_scr.rearrange("(h p) c -> p h c", p=P)[:, :, P + i * TW:P + (i + 1) * TW],
                       s_x[s])
        cl_ld[ci] = t2

    issue_cload(0)
    issue_cload(1)
    tok_wco = k.dma("sp", wco_s[:].rearrange("p a b -> p (a b)"), wview("wcout", DC * D), s_res2)
    gcs_free = [None, None]
    zT_free = None
    ytmp_free = None
    for ci, (kind, i) in enumerate(clist):
        s = ci % 2
        W = P if kind == "h" else TW
        xv = B.xc[s]
        OT = otc[s]
        k.wait("pe", tok_resC)
        k.wait("pe", cl_ld[ci])
        k.wait("dve", cl_ld[ci])
        x_done = None
        pending = None
        for do in range(DC):
            A = acc.tiles[acc.n % acc.nb]
            k.wait("pe", acc.prev_free()); nA = acc.n; acc.n += 1
            for hc in range(NH):
                ins = pe.matmul(A[:, :W], lhsT=wo_s[:, hc, do * P:(do + 1) * P], rhs=OT[:, hc, :W], start=(hc == 0),
                                stop=(hc == NH - 1))
            t_pe = k.sig("pe", ins)
            if pending is not None:
                pending()
            k.wait("dve", t_pe)
            x_done = k.sig("dve", dve.tensor_tensor(out=xv[:, do, :W], in0=A[:, :W], in1=xv[:, do, :W], op=ALU.add))
            acc.free[nA] = x_done
            pending = stats_step(xv, W, do, x_done)
        ss_n = pending()
        xslot_free[s].append(t_pe)
        x_done, ss_n = ffn(xv, W, x_done, "wgu02", "wd02", 2, ss_tok=ss_n)
        x_done, ss_n = ffn(xv, W, x_done, "wgu11", "wd11", 3, ss_tok=ss_n)
        wc_src = wview("wcin", 3 * DC * P)
        npre = min(B.wgu_r.nb, DC)
        if kind == "h":
            wc_sel = lambda d_: wc_src[d_ * P:(d_ + 1) * P, DC * P:3 * DC * P]
            wc_dst = lambda t: t[:, 1:3, :, :].rearrange("p a b c -> p (a b c)")
        else:
            wc_sel = lambda d_: wc_src[d_ * P:(d_ + 1) * P, :]
            wc_dst = lambda t: t[:].rearrange("p a b c -> p (a b c)")
        for dc in range(npre):
            B.wgu_r.load(wc_sel(dc), wc_dst)
        h_toks = norm(xv, W, 4, x_done, ss_tok=ss_n)
        t_h = h_toks[-1]
        last_pe = None
        t_z = None
        for dc in range(DC):
            n, wt = B.wgu_r.take()
            k.wait("pe", t_h)
            accs = []
            nsub = 2 if kind == "h" else 3
            for si in ([1, 2] if kind == "h" else [1, 2, 0]):
                A = acc.tiles[acc.n % acc.nb]
                k.wait("pe", acc.prev_free()); nA = acc.n; acc.n += 1
                for kc in range(DC):
                    ins = pe.matmul(A[:, :W], lhsT=wt[:, si, kc, :], rhs=B.hT[:, kc, :W], start=(kc == 0),
                                    stop=(kc == DC - 1))
                accs.append((A, nA, k.sig("pe", ins)))
            last_pe = accs[-1][2]
            B.wgu_r.rel[n] = last_pe
            if dc + npre < DC:
                d2 = dc + npre
                B.wgu_r.load(wc_sel(d2), wc_dst)
            gs = dc % 2
            (GCa, nGC, tGC), (UUa, nUU, tUU) = accs[0], accs[1]
            k.wait("act", tGC)
            k.wait("act", gcs_free[gs])
            t_a = k.sig("act", act.copy(out=gcs[gs][:, :W], in_=GCa[:, :W]))
            acc.free[nGC] = t_a
            k.wait("dve", t_a)
            k.wait("dve", tUU)
            t_u = k.sig("dve", dve.tensor_tensor(out=ubuf[:, dc, 2:2 + W], in0=gcs[gs][:, :W], in1=UUa[:, :W],
                                                 op=ALU.mult))
            acc.free[nUU] = t_u
            gcs_free[gs] = t_u
            if kind == "h":
                k.wait("dve", t_u)
                t_u = k.sig("dve", dve.tensor_scalar(out=ubuf[:, dc, 0:2], in0=ubuf[:, dc, W:W + 2],
                                                     scalar1=halo_on[:, 0:1], scalar2=None, op0=ALU.mult))
                continue
            GBa, nGB, tGB = accs[2]
            dve.tensor_scalar(out=ytmp[:, :W], in0=ubuf[:, dc, 2:2 + W], scalar1=convw[:, 2, dc:dc + 1], scalar2=None,
                              op0=ALU.mult)
            dve.scalar_tensor_tensor(out=ytmp[:, :W], in0=ubuf[:, dc, 1:1 + W], scalar=convw[:, 1, dc:dc + 1],
                                     in1=ytmp[:, :W], op0=ALU.mult, op1=ALU.add)
            dve.scalar_tensor_tensor(out=ytmp[:, :W], in0=ubuf[:, dc, 0:W], scalar=convw[:, 0, dc:dc + 1],
                                     in1=ytmp[:, :W], op0=ALU.mult, op1=ALU.add)
            dve.tensor_copy(out=ubuf[:, dc, 0:2], in_=ubuf[:, dc, W:W + 2])
            k.wait("dve", tGB)
            if dc == 0:
                k.wait("dve", zT_free)
            t_z = k.sig("dve", dve.tensor_tensor(out=zT[:, dc, :W], in0=ytmp[:, :W], in1=GBa[:, :W], op=ALU.mult))
            acc.free[nGB] = t_z
        st["hT_free"] = last_pe
        if kind == "h":
            xslot_free[s].append(t_h)
            if ci + 2 < len(clist):
                issue_cload(ci + 2)
            continue
        k.wait("pe", t_z)
        k.wait("pe", tok_wco)
        pending = None
        for do in range(DC):
            A = acc.tiles[acc.n % acc.nb]
            k.wait("pe", acc.prev_free()); nA = acc.n; acc.n += 1
            for kc in range(DC):
                ins = pe.matmul(A[:, :W], lhsT=wco_s[:, kc, do * P:(do + 1) * P], rhs=zT[:, kc, :W], start=(kc == 0),
                                stop=(kc == DC - 1))
            t_pe = k.sig("pe", ins)
            if pending is not None:
                pending()
            k.wait("dve", t_pe)
            x_done = k.sig("dve", dve.tensor_tensor(out=xv[:, do, :W], in0=A[:, :W], in1=xv[:, do, :W], op=ALU.add))
            acc.free[nA] = x_done
            pending = stats_step(xv, W, do, x_done)
        ss_n = pending()
        zT_free = t_pe
        x_done, ss_n = ffn(xv, W, x_done, "wgu12", "wd12", 5, ss_tok=ss_n)
        t_r = norm(xv, W, 6, x_done, write_h=False, ss_tok=ss_n)[-1]
        for dc in range(DC):
            ins = dve.scalar_tensor_tensor(out=xv[:, dc, :W], in0=xv[:, dc, :W], scalar=gains[:, 6, dc:dc + 1],
                                           in1=B.rstd[:, :W], op0=ALU.mult, op1=ALU.mult)
        t_o = k.sig("dve", ins)
        st["rstd_free"] = t_o
        k.wait("act", t_o)
        t_st = k.dma("act", out_d[i * P:(i + 1) * P, :], xv[:].rearrange("p a b -> p (a b)"), s_out[s])
        xslot_free[s].append(t_st)
        if ci + 2 < len(clist):
            issue_cload(ci + 2)
    k.barrier()
    phaseC.close()
    es_close(k, es)
    return nc


def es_close(k, es):
    toks = [(s, s.v) for s in k.dma_sems]
    for t in toks:
        k.wait("sp", t)
        k.wait("act", t)
    es.close()


def _chunks_fm(xh):
    a = xh.reshape(NCH, TW, DC, P).transpose(0, 3, 2, 1)
    return np.ascontiguousarray(a).reshape(NCH * P, DC * TW)


def _unchunks_fm(o):
    a = o.reshape(NCH, P, DC, TW).transpose(0, 3, 2, 1)
    return np.ascontiguousarray(a).reshape(HALF, D)


def _vecfm(g):
    return np.ascontiguousarray(g.reshape(DC, P).T)


def _flat(a):
    return np.ascontiguousarray(a, dtype=np.float32).reshape(-1, 2048)


def _prep_weights(inp):
    w = {}
    for l in range(2):
        for f in (1, 2):
            wg = inp[f"ffn{f}_w_gate"][l].reshape(DC, P, FC, P).transpose(2, 1, 0, 3)
            wu = inp[f"ffn{f}_w_up"][l].reshape(DC, P, FC, P).transpose(2, 1, 0, 3)
            w[f"wgu{l}{f}"] = _flat(np.stack([wg, wu], axis=2))
            w[f"wd{l}{f}"] = _flat(inp[f"ffn{f}_w_down"][l].reshape(FC, P, DC, P).transpose(2, 1, 0, 3))
    wqkv = inp["attn_w_qkv"][0]
    w["wqk"] = _flat(wqkv[:, :2048].reshape(DC, P, 16, P).transpose(2, 1, 0, 3))
    w["wv"] = _flat(wqkv[:, 2048:].reshape(DC, P, D).transpose(1, 0, 2))
    w["wo"] = _flat(inp["attn_w_out"][0].reshape(DC, P, D).transpose(1, 0, 2))
    w["wcin"] = _flat(inp["conv_w_in"][0].reshape(DC, P, 3, DC, P).transpose(3, 1, 2, 0, 4))
    w["wcout"] = _flat(inp["conv_w_out"][0].reshape(DC, P, D).transpose(1, 0, 2))
    return w


def _consts():
    ident = np.eye(P, dtype=np.float32)
    p = np.arange(P)[:, None, None]
    m = np.arange(4)[None, :, None]
    f = np.arange(TW)[None, None, :]
    masks = (f - p - P * m >= 0).astype(np.float32).reshape(P, 4 * TW)
    return ident, masks


def make_in_maps(inp):
    inp = {k_: np.asarray(v, dtype=np.float32) for k_, v in inp.items()}
    w = _prep_weights(inp)
    ident, masks = _consts()
    gains = np.stack([_vecfm(inp["ffn1_norm"][0]), _vecfm(inp["mix_norm"][0]), _vecfm(inp["ffn2_norm"][0]),
                      _vecfm(inp["ffn1_norm"][1]), _vecfm(inp["mix_norm"][1]), _vecfm(inp["ffn2_norm"][1]),
                      _vecfm(inp["final_norm"])], axis=1).reshape(P, 7 * DC)
    convw = np.stack([_vecfm(inp["conv_w"][0][j]) for j in range(3)], axis=1).reshape(P, 3 * DC)
    lamv = np.concatenate([inp["attn_lambda_q1"][0], inp["attn_lambda_k1"][0], inp["attn_lambda_q2"][0],
                           inp["attn_lambda_k2"][0]]).reshape(1, 256)
    subln = inp["attn_subln"][0].reshape(1, P)
    x = inp["x"]
    maps = []
    for core in range(8):
        b, r = core // 2, core % 2
        own = x[b, r * HALF:(r + 1) * HALF]
        r0 = x[b, 0:HALF]
        vis = np.ones((P, NKT), np.float32)
        if r == 0:
            vis[:, :32] = 0.0
        m = {"x_r0": _chunks_fm(r0), "x_own": _chunks_fm(own), "gains": np.ascontiguousarray(gains),
             "convw": np.ascontiguousarray(convw), "vis": vis, "halo_on": np.full((P, 1), float(r), np.float32),
             "lamv": np.ascontiguousarray(lamv), "subln": np.ascontiguousarray(subln), "ident": ident,
             "masks": masks}
        m.update(w)
        maps.append(m)
    return maps


_NC_CACHE = {}


def kernel(**inputs):
    if "nc" not in _NC_CACHE:
        _NC_CACHE["nc"] = build_program()
    nc = _NC_CACHE["nc"]
    maps = make_in_maps(inputs)
    res = run_bass_kernel_spmd(nc, maps, core_ids=list(range(8)))
    out = np.empty((4, 2 * HALF, D), np.float32)
    for core in range(8):
        b, r = core // 2, core % 2
        out[b, r * HALF:(r + 1) * HALF] = _unchunks_fm(np.asarray(res.results[core]["out"]))
    return out
```
